# Optimizing a Trainium2 kernel written in Bass

```python
import jax, jax.numpy as jnp
from jax import lax
import numpy as np

D_MODEL = 1024
BATCH = 16
SEQ = 2048
DEPTH = 1

D_RWKV = 512
HEAD_SIZE = 64
N_RWKV_HEADS = D_RWKV // HEAD_SIZE
D_CONV = D_MODEL - D_RWKV
CONV_WIDTH = 31
D_DECAY_LORA = 64
D_AAA_LORA = 64
D_GATE_LORA = 128
D_FF = 4 * D_MODEL
RMS_EPS = 1e-6
GN_EPS = 64e-5
LN_EPS = 1e-5
D_RWKV_IN = 3 * D_RWKV + D_DECAY_LORA + D_AAA_LORA + D_GATE_LORA
D_IN = D_RWKV_IN + 2 * D_CONV

kernel_name = "hymba_rwkv7_conformer_conv_hybrid"


def rmsnorm(x, g):
    xf = x.astype(jnp.float32)
    y = xf * lax.rsqrt(jnp.mean(xf * xf, axis=-1, keepdims=True) + RMS_EPS)
    return (y * g.astype(jnp.float32)).astype(x.dtype)


def token_shift(p):
    return jnp.pad(p[:, :-1], ((0, 0), (1, 0), (0, 0)))


def rwkv7_recurrence(r, decay, k, v, a_vec, b_vec):
    B, T, H, N = r.shape
    xs = tuple(jnp.swapaxes(t, 0, 1) for t in (r, decay, k, v, a_vec, b_vec))

    def step(S, inp):
        r_t, w_t, k_t, v_t, a_t, b_t = inp
        sa = jnp.einsum('bhvk,bhk->bhv', S, a_t)
        S = S * w_t[:, :, None, :] + sa[..., :, None] * b_t[:, :, None, :] + v_t[..., :, None] * k_t[:, :, None, :]
        y = jnp.einsum('bhvk,bhk->bhv', S, r_t)
        return S, y

    S0 = jnp.zeros((B, H, N, N), jnp.float32)
    _, ys = lax.scan(step, S0, xs)
    return jnp.swapaxes(ys, 0, 1)


def rwkv7_group(p, w0, w_decay_up, a0, w_aaa_up, w_gate_up, k_k, k_a, r_k, ln_x_g, ln_x_b):
    B, T, _ = p.shape
    o = 0
    r = p[..., o:o + D_RWKV]; o += D_RWKV
    k = p[..., o:o + D_RWKV]; o += D_RWKV
    v = p[..., o:o + D_RWKV]; o += D_RWKV
    w_down = p[..., o:o + D_DECAY_LORA]; o += D_DECAY_LORA
    a_down = p[..., o:o + D_AAA_LORA]; o += D_AAA_LORA
    g_down = p[..., o:o + D_GATE_LORA]

    w_raw = -jax.nn.softplus(-(w0 + jnp.tanh(w_down) @ w_decay_up).astype(jnp.float32)) - 0.5
    decay = jnp.exp(-jnp.exp(w_raw))
    a = jax.nn.sigmoid(a0 + a_down @ w_aaa_up)
    g = jax.nn.sigmoid(g_down) @ w_gate_up

    heads = lambda t: t.reshape(B, T, N_RWKV_HEADS, HEAD_SIZE).astype(jnp.float32)
    kk = heads(k * k_k)
    kk = kk / jnp.maximum(jnp.linalg.norm(kk, axis=-1, keepdims=True), 1e-12)
    k = k * (1.0 + (a - 1.0) * k_a)

    rh, kh, vh, ah = heads(r), heads(k), heads(v), heads(a)
    y = rwkv7_recurrence(rh, heads(decay), kh, vh, -kk, kk * ah)

    mu = jnp.mean(y, axis=-1, keepdims=True)
    var = jnp.mean(jnp.square(y - mu), axis=-1, keepdims=True)
    y = (y - mu) * lax.rsqrt(var + GN_EPS)
    y = y.reshape(B, T, D_RWKV) * ln_x_g + ln_x_b
    bonus = jnp.sum(rh * kh * heads(jnp.broadcast_to(r_k, r.shape)), axis=-1, keepdims=True) * vh
    y = y + bonus.reshape(B, T, D_RWKV)
    return (y * g).astype(p.dtype)


def conformer_conv_group(u, conv_w, conv_b, conv_ln_g, conv_ln_b):
    h = u[..., :D_CONV] * jax.nn.sigmoid(u[..., D_CONV:])
    h = lax.conv_general_dilated(
        h, conv_w[:, None, :].astype(h.dtype), window_strides=(1,),
        padding=[(CONV_WIDTH - 1, 0)],
        dimension_numbers=('NWC', 'WIO', 'NWC'),
        feature_group_count=D_CONV) + conv_b
    hf = h.astype(jnp.float32)
    mu = jnp.mean(hf, axis=-1, keepdims=True)
    var = jnp.mean(jnp.square(hf - mu), axis=-1, keepdims=True)
    hn = (hf - mu) * lax.rsqrt(var + LN_EPS) * conv_ln_g + conv_ln_b
    return jax.nn.silu(hn).astype(u.dtype)


def setup_inputs(seed: int = 0) -> dict:
    key = jax.random.key(seed)
    ks = jax.random.split(key, 24)
    f32 = jnp.float32
    nrm = lambda k, shape, s: jax.random.normal(k, shape, f32) * s
    L = DEPTH
    decay_base = jnp.linspace(-6.5, -1.5, D_RWKV, dtype=f32)
    return {
        "x": jax.random.normal(ks[0], (BATCH, SEQ, D_MODEL), f32),
        "norm_mix_g": 1.0 + nrm(ks[1], (L, D_MODEL), 0.02),
        "w_in": nrm(ks[2], (L, D_MODEL, D_IN), D_MODEL ** -0.5),
        "shift_mu": jax.random.uniform(ks[3], (L, D_RWKV_IN), f32, 0.2, 0.8),
        "w0": decay_base[None, :] + nrm(ks[4], (L, D_RWKV), 0.1),
        "w_decay_up": nrm(ks[5], (L, D_DECAY_LORA, D_RWKV), 0.3 * D_DECAY_LORA ** -0.5),
        "a0": nrm(ks[6], (L, D_RWKV), 0.1),
        "w_aaa_up": nrm(ks[7], (L, D_AAA_LORA, D_RWKV), 0.3 * D_AAA_LORA ** -0.5),
        "w_gate_up": nrm(ks[8], (L, D_GATE_LORA, D_RWKV), D_GATE_LORA ** -0.5),
        "k_k": 0.85 + nrm(ks[9], (L, D_RWKV), 0.02),
        "k_a": 1.0 + nrm(ks[10], (L, D_RWKV), 0.02),
        "r_k": nrm(ks[11], (L, D_RWKV), 0.1),
        "ln_x_g": 1.0 + nrm(ks[12], (L, D_RWKV), 0.02),
        "ln_x_b": nrm(ks[13], (L, D_RWKV), 0.01),
        "conv_w": nrm(ks[14], (L, CONV_WIDTH, D_CONV), CONV_WIDTH ** -0.5),
        "conv_b": nrm(ks[15], (L, D_CONV), 0.01),
        "conv_ln_g": 1.0 + nrm(ks[16], (L, D_CONV), 0.02),
        "conv_ln_b": nrm(ks[17], (L, D_CONV), 0.01),
        "w_out": nrm(ks[18], (L, D_MODEL, D_MODEL), D_MODEL ** -0.5),
        "norm_mlp_g": 1.0 + nrm(ks[19], (L, D_MODEL), 0.02),
        "w_ff1": nrm(ks[20], (L, D_MODEL, D_FF), D_MODEL ** -0.5),
        "w_ff2": nrm(ks[21], (L, D_FF, D_MODEL), D_FF ** -0.5),
        "norm_final_g": 1.0 + nrm(ks[22], (D_MODEL,), 0.02),
    }


def reference(x, norm_mix_g, w_in, shift_mu, w0, w_decay_up, a0, w_aaa_up, w_gate_up,
              k_k, k_a, r_k, ln_x_g, ln_x_b, conv_w, conv_b, conv_ln_g, conv_ln_b,
              w_out, norm_mlp_g, w_ff1, w_ff2, norm_final_g):
    for l in range(DEPTH):
        h = rmsnorm(x, norm_mix_g[l])
        p = h @ w_in[l]
        p_rwkv = p[..., :D_RWKV_IN]
        p_rwkv = p_rwkv + (token_shift(p_rwkv) - p_rwkv) * shift_mu[l]
        p_conv = p[..., D_RWKV_IN:]
        y_rwkv = rwkv7_group(p_rwkv, w0[l], w_decay_up[l], a0[l], w_aaa_up[l], w_gate_up[l],
                             k_k[l], k_a[l], r_k[l], ln_x_g[l], ln_x_b[l])
        y_conv = conformer_conv_group(p_conv, conv_w[l], conv_b[l], conv_ln_g[l], conv_ln_b[l])
        x = x + jnp.concatenate([y_rwkv, y_conv], axis=-1) @ w_out[l]
        h = rmsnorm(x, norm_mlp_g[l])
        x = x + jnp.square(jax.nn.relu(h @ w_ff1[l])) @ w_ff2[l]
    return rmsnorm(x, norm_final_g)
```

```python
import contextlib
import numpy as np
import concourse.bass as bass
import concourse.mybir as mybir
from concourse.bass_utils import run_bass_kernel_spmd

F32 = mybir.dt.float32
BF16 = mybir.dt.bfloat16
AF = mybir.ActivationFunctionType
ALU = mybir.AluOpType

NCORES = 8
D = 1024
SEQ = 2048
NSEQ = 2
TOK = NSEQ * SEQ
BLK = 512
NBLK = SEQ // BLK
DIN = 2816
DFF = 4096
CW = 31
C0 = float(np.exp(-0.5))
RMS_EPS = 1e-6
GN_EPS = 64e-5
LN_EPS = 1e-5

NCSTB = 2688
PC_MU, PC_W0, PC_A0, PC_KK, PC_KA, PC_RK, PC_LXG, PC_LXB, PC_CB, PC_CLG, PC_CLB = 0, 14, 18, 22, 26, 30, 34, 38, 42, 46, 50
NPCOL = 54


class T:
    __slots__ = ("name", "w", "r")

    def __init__(self, name):
        self.name = name
        self.w = None
        self.r = []


class Op:
    __slots__ = ("eng", "fn", "deps", "pos", "ticket", "signal", "dkey", "dseq", "waits")


class Sched:
    ENGS = ("pe", "act", "dve", "pool", "sp")

    def __init__(self):
        self.per = {e: [] for e in self.ENGS}
        self.ops = []
        self.dcount = {}

    def op(self, eng, fn, reads=(), writes=(), dkey=None):
        o = Op()
        o.eng = eng
        o.fn = fn
        o.signal = False
        o.dkey = dkey
        o.dseq = 0
        o.ticket = 0
        deps = set()
        for t in reads:
            if t.w is not None:
                deps.add(t.w)
        for t in writes:
            if t.w is not None:
                deps.add(t.w)
            deps.update(t.r)
        for t in reads:
            t.r.append(o)
        for t in writes:
            t.w = o
            t.r = []
        deps.discard(o)
        if dkey is not None and dkey.startswith(("init", "all:")):
            deps = {d for d in deps if d.dkey != dkey}
        o.deps = deps
        o.pos = len(self.per[eng])
        self.per[eng].append(o)
        self.ops.append(o)
        if dkey is not None:
            c = self.dcount.get(dkey, 0) + 1
            self.dcount[dkey] = c
            o.dseq = c
        return o

    @staticmethod
    def _skip(d, o):
        return d.dkey is None and o.dkey is None and d.eng == "pe" and o.eng == "pe"

    def finalize(self):
        for o in self.ops:
            for d in o.deps:
                if d.dkey is None and not self._skip(d, o):
                    d.signal = True
        for e in self.ENGS:
            n = 0
            for o in self.per[e]:
                if o.dkey is None and o.signal:
                    n += 1
                    o.ticket = n
        for e in self.ENGS:
            known = {}
            for o in self.per[e]:
                w = {}
                for d in o.deps:
                    if d.dkey is not None:
                        key = ("dma", d.dkey)
                        val = 16 * (self.dcount[d.dkey] if d.dkey.startswith(("init", "all:")) else d.dseq)
                    else:
                        if self._skip(d, o):
                            continue
                        key = ("eng", d.eng)
                        val = d.ticket
                    if known.get(key, 0) >= val:
                        continue
                    if w.get(key, 0) < val:
                        w[key] = val
                for k, v in w.items():
                    known[k] = v
                o.waits = list(w.items())

    def emit(self, eng, e, sems):
        for o in self.per[eng]:
            for key, val in o.waits:
                e.wait_ge(sems[key], val)
            if o.fn is None:
                continue
            inst = o.fn(e)
            if o.dkey is not None:
                inst.then_inc(sems[("dma", o.dkey)], 16)
            elif o.signal:
                inst.then_inc(sems[("eng", o.eng)], 1)


def build_program(stage=99, debug_cols=0):
    nc = bass.Bass("TRN2", target_bir_lowering=False)
    S = Sched()
    es = contextlib.ExitStack()

    def dram(name, shape, dt=F32, kind="ExternalInput"):
        return nc.dram_tensor(name, list(shape), dt, kind=kind).ap()

    x_d = dram("x", [TOK, D])
    w_in_d = dram("w_in", [D, DIN])
    w_out_d = dram("w_out", [D, D])
    w_ff1_d = dram("w_ff1", [D, DFF])
    w_ff2_d = dram("w_ff2", [DFF, D])
    lora_d = dram("lora_up", [128, 512])
    gate_d = dram("w_gate_up", [128, 512])
    pcol_d = dram("pcol", [128, NPCOL])
    cw_d = dram("cw", [128, 4 * CW])
    gvec_d = dram("gvec", [3, D])
    cst_d = dram("cst", [128, NCSTB])
    out_d = dram("out", [TOK, D], kind="ExternalOutput")
    x1_d = dram("x1_scratch", [TOK, D], kind="Internal")
    dbg_d = dram("dbg", [128, max(debug_cols, 8)], kind="ExternalOutput") if debug_cols else None
    dbg_off = [0]

    ARENA = 212000
    big = es.enter_context(nc.sbuf_tensor("arena", [128, ARENA // 2], BF16))
    top = [0]

    def sb(name, shape, dt=F32):
        n = 1
        for v in shape[1:]:
            n *= v
        nbytes = n * (4 if dt == F32 else 2)
        off = (top[0] + 63) // 64 * 64
        assert off + nbytes <= ARENA, ("SBUF arena overflow", name, off + nbytes)
        top[0] = off + nbytes
        v = big[:, off // 2:(off + nbytes) // 2]
        if dt == F32:
            v = v.bitcast(F32)
        if len(shape) == 3:
            v = v.rearrange("p (a b) -> p a b", a=shape[1])
        elif len(shape) == 4:
            v = v.rearrange("p (a b c) -> p a b c", a=shape[1], b=shape[2])
        elif len(shape) == 5:
            v = v.rearrange("p (a b c d) -> p a b c d", a=shape[1], b=shape[2], c=shape[3])
        return v

    def W(name, shape=(128, BLK), dt=F32):
        return sb(name, list(shape), dt), T(name)

    import os
    SKD = os.environ.get("SKIPD", "").split(",")
    dn = [0]
    def dma(eng, out, in_, reads, writes, dkey, **kw):
        if dkey in SKD or (eng + ":" + dkey) in SKD:
            return
        dn[0] += 1
        if str(dn[0]) in SKD:
            return
        if dkey.startswith(("init", "all:")):
            dkey = dkey + "_" + eng
        S.op(eng, lambda e: e.dma_start(out=out, in_=in_, **kw), reads=reads, writes=writes, dkey=dkey)

    dbg_names = {}

    def dump(name, ap_sb, tt, ncols, bf=False):
        o = dbg_off[0]
        dbg_off[0] += ncols
        dbg_names[name] = (o, ncols)
        dma("pool" if bf else "sp", dbg_d[:, o:o + ncols], ap_sb, [tt], [], "all:dbg")

    def barrier():
        deps = set()
        lastd = {}
        for o in S.ops:
            if o.dkey is not None:
                lastd[o.dkey] = o
        deps.update(lastd.values())
        for e_ in Sched.ENGS:
            for o in reversed(S.per[e_]):
                if o.dkey is None and o.fn is not None:
                    deps.add(o)
                    break
        for e_ in Sched.ENGS:
            o = S.op(e_, None)
            o.deps = set(deps)

    PB = {}
    tPB = {}
    for i in (0, 1, 3, 4, 5, 6, 7):
        PB[i] = es.enter_context(nc.psum_tensor("pb%d" % i, [128, 512], F32))
        tPB[i] = T("pb%d" % i)
    PTB = es.enter_context(nc.psum_tensor("ptb", [128, 1024], BF16))
    tPTB = T("ptb")
    big_i = [0]

    def big_bank():
        big_i[0] ^= 1
        return big_i[0]

    rec_i = [0]

    REC_RING = (4, 5, 6, 7)

    def rec_bank():
        rec_i[0] = (rec_i[0] + 1) % len(REC_RING)
        return REC_RING[rec_i[0]]

    cstb = sb("cstb", [128, NCSTB], BF16)
    pcol = sb("pcol", [128, NPCOL])
    omm = sb("omm", [128, 14])
    cw_sb = sb("cw_sb", [128, 4 * CW])
    XS = [sb("xs%d" % i, [128, D]) for i in range(2)]
    tXS = [T("xs%d" % i) for i in range(2)]
    hb = [sb("hb%d" % i, [128, D], BF16) for i in range(2)]
    t_hb = [T("hb%d" % i) for i in range(2)]
    ss4 = sb("ss4", [128, 8])
    t_ss = [T("ss%d" % i) for i in range(8)]
    hT = sb("hT", [128, 8, BLK], BF16)
    t_hT = [T("hT%d" % j) for j in range(4)]
    t_par = T("params")
    t_cst = T("cst_derived")
    PERSIST_TOP = None

    ident_b = cstb[:, 0:128]
    bones_b = cstb[:, 128:256]
    aones_b = cstb[:, 256:384]
    ident4_b = cstb[:, 384:896]
    maskT = cstb[:, 896:1408]
    maskL = cstb[:, 1408:1920]
    istack = cstb[:, 1920:2176]
    ones_b = cstb[:, 2176:2688]

    dma("sp", pcol, pcol_d, [], [t_par], "init")
    dma("sp", cw_sb, cw_d, [], [t_par], "init")
    dma("pool", cstb, cst_d, [], [t_par], "init", max_dma_last_dim=4096)
    S.op("dve", lambda e: e.tensor_scalar(omm, pcol[:, PC_MU:PC_MU + 14], -1.0, 1.0, ALU.mult, ALU.add),
         [t_par], [t_cst])

    xs_n = [0]

    def rms_rstd(cj, t_col, n, eps):
        S.op("dve", lambda e: e.tensor_scalar(cj, cj, 1.0 / n, eps, ALU.mult, ALU.add), [t_col], [t_col])
        S.op("act", lambda e: e.activation(cj, cj, AF.Ln), [t_col], [t_col])
        S.op("act", lambda e: e.activation(cj, cj, AF.Exp, scale=-0.5), [t_col], [t_col])

    def load_norm_transpose(tok0, gvec_ap, t_g, src_d, ntiles, keep=None, hoff=0, tbase=0):
        for j in range(ntiles):
            r0 = tok0 + j * 128
            if keep is None:
                sl = xs_n[0] % 2
                xs_n[0] += 1
                xt, txt, key = XS[sl], tXS[sl], "xs%d" % sl
            else:
                xt, txt, key = keep[0][j], keep[1][j], keep[2] + str(j)
            dma("sp", xt, src_d[r0:r0 + 128, :], [], [txt], key)
            cj = ss4[:, j:j + 1]
            sl2 = j % 2
            S.op("act", lambda e, xt=xt, cj=cj, sl2=sl2: e.activation(hb[sl2], xt, AF.Square, accum_out=cj),
                 [txt], [t_hb[sl2], t_ss[j]])
            rms_rstd(cj, t_ss[j], D, RMS_EPS)
            S.op("dve", lambda e, xt=xt, cj=cj, sl2=sl2: e.scalar_tensor_tensor(hb[sl2], xt, cj, gvec_ap, ALU.mult, ALU.mult),
                 [txt, t_ss[j], t_g], [t_hb[sl2]])
            for k in range(8):
                S.op("pe", lambda e, sl2=sl2, k=k: e.transpose(PTB[:, k * 128:(k + 1) * 128],
                                                               hb[sl2][:, k * 128:(k + 1) * 128], ident_b),
                     [t_hb[sl2], t_par], [tPTB])
            S.op("act", lambda e, j=j: e.copy(hT[:, :, hoff + j * 128:hoff + (j + 1) * 128],
                                              PTB[:, :].rearrange("p (k t) -> p k t", k=8)),
                 [tPTB], [t_hT[tbase + j]])


    def front_gen(tok0, gvec_ap, t_g, src_d, wait_flag=None):
        while wait_flag is not None and not wait_flag[0]:
            yield
        tiles = []
        for j in range(5):
            if j < 4:
                r0 = tok0 + j * 128
                sl = xs_n[0] % 2
                xs_n[0] += 1
                xt, txt = XS[sl], tXS[sl]
                dma("sp", xt, src_d[r0:r0 + 128, :], [], [txt], "xs%d" % sl)
                cj = ss4[:, j:j + 1]
                sl2 = j % 2
                S.op("act", lambda e, xt=xt, cj=cj, sl2=sl2: e.activation(hb[sl2], xt, AF.Square, accum_out=cj),
                     [txt], [t_hb[sl2], t_ss[j]])
                yield
                rms_rstd(cj, t_ss[j], D, RMS_EPS)
                yield
                S.op("dve", lambda e, xt=xt, cj=cj, sl2=sl2: e.scalar_tensor_tensor(hb[sl2], xt, cj, gvec_ap, ALU.mult, ALU.mult),
                     [txt, t_ss[j], t_g], [t_hb[sl2]])
                yield
            if j > 0:
                jj = j - 1
                sl2 = jj % 2
                for k in range(8):
                    S.op("pe", lambda e, sl2=sl2, k=k: e.transpose(PTB[:, k * 128:(k + 1) * 128],
                                                                   hb[sl2][:, k * 128:(k + 1) * 128], ident_b),
                         [t_hb[sl2], t_par], [tPTB])
                S.op("act", lambda e, jj=jj: e.copy(hT[:, :, jj * 128:(jj + 1) * 128],
                                                    PTB[:, :].rearrange("p (k t) -> p k t", k=8)),
                     [tPTB], [t_hT[jj]])
            yield

    persist_top = top[0]

    w_in_sb = sb("w_in_sb", [128, 8, DIN], BF16)
    w_out_sb = sb("w_out_sb", [128, 8, D], BF16)
    lora_sb = sb("lora_sb", [128, 512], BF16)
    gate_sb = sb("gate_sb", [128, 512], BF16)
    diagw = sb("diagw", [128, 4 * CW, 64], BF16)
    gA = sb("gA", [128, D])
    t_w_in = [T("w_in%d" % k) for k in range(8)]
    t_w_out = T("w_out")
    t_parA = T("paramsA")
    dma("sp", gA, gvec_d[0:1, :].broadcast_to([128, D]), [], [t_parA], "initA")
    dma("pool", lora_sb, lora_d, [], [t_parA], "initA")
    dma("pool", gate_sb, gate_d, [], [t_parA], "initA")
    w_in_v = w_in_d.rearrange("(k p) n -> p k n", p=128)
    import os
    SK = os.environ.get("SKIP", "")
    WPARTS = ((1536, 1792), (0, 1536), (1792, DIN))
    t_w_in_p = [T("w_in_part%d" % i) for i in range(3)]
    for pi, (c0, c1) in enumerate(WPARTS):
        for k in range(8):
            dma("pool", w_in_sb[:, k, c0:c1], w_in_v[:, k, c0:c1], [], [t_w_in_p[pi]], "all:w_inP%d" % pi, max_dma_last_dim=4096)

    def w_in_T(cc):
        c = cc * 128
        return t_w_in_p[0] if 1536 <= c < 1792 else (t_w_in_p[1] if c < 1536 else t_w_in_p[2])
    w_out_v = w_out_d.rearrange("(k p) n -> p k n", p=128)
    for k in range(8):
        if "wout" in SK:
            break
        dma("pool", w_out_sb[:, k, :], w_out_v[:, k, :], [], [t_w_out], "all:w_out", max_dma_last_dim=4096)
    for i in range(4 * CW):
        if "diag" in SK:
            break
        S.op("dve", lambda e, i=i: e.tensor_scalar(diagw[:, i, :], istack[:, 0:64], cw_sb[:, i:i + 1], None, ALU.mult),
             [t_par], [t_cst])

    tmpb, t_tmpb = W("tmpb", (128, BLK + 1))
    carry = sb("carry", [128, NSEQ * 14])
    t_carry = [T("carry%d" % i) for i in range(NSEQ * 14)]
    S.op("dve", lambda e: e.memset(carry, 0.0), [], t_carry)

    lw, t_lw = W("lw", dt=BF16)
    sg, t_sg = W("sg", dt=BF16)
    pr, t_pr = W("pr")
    pk, t_pk = W("pk")
    pv, t_pv = W("pv")
    mixT = sb("mixT", [128, 8, BLK], BF16)
    t_mix = [T("mix%d" % i) for i in range(8)]
    s_ = [W("s%d" % i) for i in range(8)]
    sgw, t_sgw = s_[0]
    asg, t_asg = s_[1]
    kkk, t_kkk = s_[2]
    rn, t_rn = s_[3]
    kk, t_kk = s_[4]
    t1, t_t1 = s_[3]
    kmod, t_kmod = s_[2]
    bvec, t_bvec = s_[5]
    Lc, t_Lc = s_[3]
    Lr, t_Lr = s_[1]
    Lp, t_Lp = s_[7]
    E2, t_E2 = s_[3]
    E3, t_E3 = s_[0]
    hcv = [s_[i][0] for i in range(4)]
    t_hcv = [s_[i][1] for i in range(4)]
    cmu, t_cmu = s_[4]
    cvar, t_cvar = s_[5]
    ctmp, t_ctmp = s_[6]
    sgate, t_sgate = s_[7]
    sqb, t_sqb = W("sqb", dt=BF16)
    rkr, t_rkr = W("rkr", dt=BF16)
    ybf, t_ybf = W("ybf", dt=BF16)
    ysq, t_ysq = W("ysq", dt=BF16)
    IFT = [{"din": W("din%d" % p_, (128, 4)), "AR": W("AR%d" % p_, (128, 4, 2, 128), BF16), "BT": W("BT%d" % p_, dt=BF16),
            "KT": W("KT%d" % p_, dt=BF16), "E1": W("E1_%d" % p_), "bonus": W("bonus%d" % p_)} for p_ in range(2)]
    VT, t_VT = W("VT", dt=BF16)
    TOKM, t_TOKM = W("TOKM", (128, 4, 3, 128), BF16)
    G, t_G = W("G", (128, 4, 2, 2, 64), BF16)
    yT, t_yT = W("yT")
    Y0T, t_Y0T = W("Y0T")
    RhT, t_RhT = W("RhT", dt=BF16)
    MTs, t_MTs = W("MTs", (128, 4, 64), BF16)
    Npp, t_Npp = W("Npp", (128, 4, 64))
    DTH = [{nm: W(nm + str(h), (128, 4, 128), BF16) for nm in ("XTa", "XTb", "Xa", "Xb", "Pa", "Pb", "AakT")} for h in range(2)]
    HT = [{"ArbT": W("ArbT%d" % h, (128, 4, 128), BF16), "ArkT": W("ArkT%d" % h, (128, 4, 128), BF16),
           "F": W("F%d" % h, (128, 4, 2, 64), BF16)} for h in range(2)]
    _ar0 = IFT[0]["AR"][0].rearrange("p c a t -> p (c a t)")
    hcb = [_ar0[:, 0:512], _ar0[:, 512:1024], IFT[0]["BT"][0], IFT[0]["KT"][0]]
    t_hcb = [IFT[0]["AR"][1], IFT[0]["AR"][1], IFT[0]["BT"][1], IFT[0]["KT"][1]]
    _e10 = IFT[0]["E1"][0].bitcast(BF16)
    _bo0 = IFT[0]["bonus"][0].bitcast(BF16)
    hsq = [_e10[:, 0:512], _e10[:, 512:1024], _bo0[:, 0:512], _bo0[:, 512:1024]]
    t_hsq = [IFT[0]["E1"][1], IFT[0]["E1"][1], IFT[0]["bonus"][1], IFT[0]["bonus"][1]]
    Sm = sb("Sm", [128, NSEQ * 4, 64])
    t_Sm = [T("Sm%d" % i) for i in range(NSEQ * 4)]
    Zb = sb("Zb", [128, NSEQ * 4, 64], BF16)
    t_Zb = [T("Zb%d" % i) for i in range(NSEQ * 4)]
    S.op("dve", lambda e: e.memset(Sm, 0.0), [], t_Sm)
    hglu = sb("hglu", [128, 4, 30 + BLK], BF16)
    t_hglu = [T("hglu%d" % i) for i in range(4)]
    hhist = sb("hhist", [128, NSEQ * 4, 30], BF16)
    t_hhist = [T("hhist%d" % i) for i in range(NSEQ * 4)]
    S.op("dve", lambda e: e.memset(hhist, 0.0), [], t_hhist)
    x1t, t_x1t = W("x1t", (128, D))
    gmu, t_gmu = x1t[:, 0:512], t_x1t
    gvar, t_gvar = x1t[:, 512:1024], t_x1t
    gtmp, t_gtmp = Y0T, t_Y0T
    p12, t_p12 = Y0T, t_Y0T
    print("phase A SBUF top:", top[0])

    def col(base, i):
        return pcol[:, base + i:base + i + 1]

    def c4v(ap):
        return ap.rearrange("p (c t) -> p c t", c=4)

    def inproj_chunk(cc):
        b = big_bank()
        for k in range(8):
            S.op("pe", lambda e, k=k, b=b: e.matmul(PB[b][:, :], w_in_sb[:, k, cc * 128:(cc + 1) * 128], hT[:, k, :],
                                                     start=(k == 0), stop=(k == 7)),
                 [w_in_T(cc)] + t_hT, [tPB[b]])
        return b

    def shift_evac(b, cc, s, dst, t_dst):
        ci = s * 14 + cc
        S.op("act", lambda e: e.activation(tmpb[:, 1:BLK + 1], PB[b][:, :], AF.Copy, scale=pcol[:, PC_MU + cc:PC_MU + cc + 1]),
             [tPB[b], t_par], [t_tmpb])
        S.op("act", lambda e: e.copy(tmpb[:, 0:1], carry[:, ci:ci + 1]), [t_carry[ci], t_tmpb], [t_tmpb])
        S.op("act", lambda e: e.copy(carry[:, ci:ci + 1], tmpb[:, BLK:BLK + 1]), [t_tmpb], [t_carry[ci]])
        S.op("dve", lambda e: e.scalar_tensor_tensor(dst, PB[b][:, :], omm[:, cc:cc + 1], tmpb[:, 0:BLK], ALU.mult, ALU.add),
             [tPB[b], t_tmpb, t_cst], [t_dst])

    PS = 3

    def make_pair(par):
        AR, t_AR = IFT[par]['AR']
        BT, t_BT = IFT[par]['BT']
        KT, t_KT = IFT[par]['KT']
        E1, t_E1 = IFT[par]['E1']
        din, t_din = IFT[par]['din']
        bonus, t_bonus = IFT[par]['bonus']

        def rwkv_pair_prep(s, hp):
            for cc, dst, tdst in ((hp, pr, t_pr), (4 + hp, pk, t_pk), (8 + hp, pv, t_pv)):
                b = inproj_chunk(cc)
                shift_evac(b, cc, s, dst, tdst)
                yield
            S.op("pe", lambda e: e.matmul(PB[PS][:, :], lora_sb[0:64, hp * 128:(hp + 1) * 128], lw[0:64, :], start=True, stop=True),
                 [t_parA, t_lw], [tPB[PS]])
            S.op("act", lambda e: e.activation(sgw, PB[PS][:, :], AF.Sigmoid, bias=col(PC_W0, hp)), [tPB[PS], t_par], [t_sgw])
            yield
            S.op("pe", lambda e: e.matmul(PB[PS][:, :], lora_sb[64:128, hp * 128:(hp + 1) * 128], lw[64:128, :], start=True, stop=True),
                 [t_parA, t_lw], [tPB[PS]])
            S.op("act", lambda e: e.activation(asg, PB[PS][:, :], AF.Sigmoid, bias=col(PC_A0, hp)), [tPB[PS], t_par], [t_asg])
            yield
            S.op("dve", lambda e: e.tensor_scalar(kkk, pk, col(PC_KK, hp), None, ALU.mult), [t_pk, t_par], [t_kkk])
            yield
            S.op("act", lambda e: e.activation(sqb, kkk, AF.Square), [t_kkk], [t_sqb])
            yield
            S.op("pe", lambda e: e.matmul(PB[PS][:, :], bones_b, sqb, start=True, stop=True), [t_par, t_sqb], [tPB[PS]])
            S.op("act", lambda e: e.activation(rn, PB[PS][:, :], AF.Ln), [tPB[PS]], [t_rn])
            yield
            S.op("act", lambda e: e.activation(rn, rn, AF.Exp, scale=-0.5), [t_rn], [t_rn])
            yield
            S.op("dve", lambda e: e.tensor_tensor(kk, kkk, rn, ALU.mult), [t_kkk, t_rn], [t_kk])
            yield
            S.op("dve", lambda e: e.tensor_scalar(t1, asg, -1.0, col(PC_KA, hp), ALU.add, ALU.mult), [t_asg, t_par], [t_t1])
            yield
            S.op("dve", lambda e: e.scalar_tensor_tensor(kmod, t1, 1.0, pk, ALU.add, ALU.mult), [t_t1, t_pk], [t_kmod])
            yield
            S.op("dve", lambda e: e.tensor_tensor(bvec, kk, asg, ALU.mult), [t_kk, t_asg], [t_bvec])
            yield
            S.op("dve", lambda e: e.scalar_tensor_tensor(rkr, pr, col(PC_RK, hp), kmod, ALU.mult, ALU.mult),
                 [t_pr, t_kmod, t_par], [t_rkr])
            yield
            S.op("pe", lambda e: e.matmul(PB[PS][:, :], bones_b, rkr, start=True, stop=True), [t_par, t_rkr], [tPB[PS]])
            S.op("dve", lambda e: e.tensor_tensor(bonus, PB[PS][:, :], pv, ALU.mult), [tPB[PS], t_pv], [t_bonus])
            yield
            S.op("dve", lambda e: e.tensor_tensor_scan(Lc, ones_b, sgw, 0.0, ALU.mult, ALU.add), [t_sgw, t_par], [t_Lc])
            yield
            for c4 in range(4):
                cs = slice(c4 * 128, (c4 + 1) * 128)
                m = c4 * 128 + 63
                S.op("dve", lambda e, cs=cs, m=m: e.tensor_scalar(Lr[:, cs], Lc[:, cs], Lc[:, m:m + 1], None, ALU.subtract),
                     [t_Lc], [t_Lr])
                yield
            S.op("dve", lambda e: e.tensor_tensor(Lp, Lr, sgw, ALU.subtract), [t_Lr, t_sgw], [t_Lp])
            yield
            S.op("act", lambda e: e.activation(E1, Lr, AF.Exp, scale=-C0), [t_Lr], [t_E1])
            yield
            S.op("act", lambda e: e.activation(E2, Lr, AF.Exp, scale=C0), [t_Lr], [t_E2])
            yield
            S.op("act", lambda e: e.activation(E3, Lp, AF.Exp, scale=-C0), [t_Lp], [t_E3])
            yield
            S.op("act", lambda e: e.activation(din, c4v(Lp)[:, :, 0], AF.Exp, scale=C0), [t_Lp], [t_din])
            yield
            S.op("dve", lambda e: e.scalar_tensor_tensor(AR[:, :, 0, :], c4v(kk), -1.0, c4v(E3), ALU.mult, ALU.mult),
                 [t_kk, t_E3], [t_AR])
            yield
            S.op("dve", lambda e: e.tensor_tensor(AR[:, :, 1, :], c4v(pr), c4v(E1), ALU.mult), [t_pr, t_E1], [t_AR])
            yield
            S.op("dve", lambda e: e.tensor_tensor(BT, bvec, E2, ALU.mult), [t_bvec, t_E2], [t_BT])
            yield
            S.op("dve", lambda e: e.tensor_tensor(KT, kmod, E2, ALU.mult), [t_kmod, t_E2], [t_KT])
            yield
            S.op("act", lambda e: e.copy(VT, pv), [t_pv], [t_VT])
            yield

        def prep_tail():
            for c4 in range(4):
                cs = slice(c4 * 128, (c4 + 1) * 128)
                srcs = [(AR[:, c4, 0, :], t_AR), (BT[:, cs], t_BT), (KT[:, cs], t_KT), (VT[:, cs], t_VT)]
                for i, (src, tsrc) in enumerate(srcs):
                    S.op("pe", lambda e, src=src, i=i: e.transpose(PTB[:, i * 128:(i + 1) * 128], src, ident_b),
                         [tsrc, t_par], [tPTB])
                S.op("act", lambda e, c4=c4: e.copy(G[:, c4, :, 0, :], PTB[:, 0:128].rearrange("p (h k) -> p h k", h=2)),
                     [tPTB], [t_G])
                S.op("act", lambda e, c4=c4: e.copy(TOKM[:, c4, :, :], PTB[:, 128:512].rearrange("p (q k) -> p q k", q=3)),
                     [tPTB], [t_TOKM])

        def f2(ap):
            return ap.rearrange("p c t -> p (c t)")

        def rec_head(s, hp, h):
            hs = slice(h * 64, (h + 1) * 64)
            DT = DTH[h]
            XT0, tXT0 = DT["XTb"]
            X0, tX0 = DT["Xb"]
            AakT, tAakT = DT["AakT"]
            ArbT, tArbT = HT[h]["ArbT"]
            ArkT, tArkT = HT[h]["ArkT"]
            Fh, tF = HT[h]["F"]
            for (lt, tlt, o0, to0, o1, to1) in ((BT, t_BT, XT0, tXT0, ArbT, tArbT), (KT, t_KT, AakT, tAakT, ArkT, tArkT)):
                for half in range(2):
                    b = rec_bank()
                    for cq in range(2):
                        c4 = half * 2 + cq
                        S.op("pe", lambda e, b=b, c4=c4, cq=cq, lt=lt: e.matmul(
                            PB[b][:, cq * 256:(cq + 1) * 256], lt[hs, c4 * 128:(c4 + 1) * 128],
                            AR[hs, c4, :, :].rearrange("p a t -> p (a t)"), start=True, stop=True),
                            [tlt, t_AR], [tPB[b]])
                    pv4 = PB[b][:, :].rearrange("p (c a t) -> p c a t", c=2, a=2)
                    mv4 = maskT.rearrange("p (c a t) -> p c a t", c=2, a=2)
                    S.op("dve", lambda e, pv4=pv4, mv4=mv4, half=half, o0=o0: e.tensor_tensor(
                        o0[:, half * 2:half * 2 + 2, :], pv4[:, :, 0, :], mv4[:, :, 0, :], ALU.mult), [tPB[b], t_par], [to0])
                    S.op("dve", lambda e, pv4=pv4, mv4=mv4, half=half, o1=o1: e.tensor_tensor(
                        o1[:, half * 2:half * 2 + 2, :], pv4[:, :, 1, :], mv4[:, :, 1, :], ALU.mult), [tPB[b], t_par], [to1])
                    yield
            b = rec_bank()
            for c4 in range(4):
                S.op("pe", lambda e, b=b, c4=c4: e.matmul(PB[b][:, c4 * 128:(c4 + 1) * 128], AR[hs, c4, 0, :],
                                                           BT[hs, c4 * 128:(c4 + 1) * 128], start=True, stop=True),
                     [t_AR, t_BT], [tPB[b]])
            S.op("dve", lambda e, b=b: e.tensor_tensor(f2(X0), PB[b][:, :], maskL, ALU.mult), [tPB[b], t_par], [tX0])
            yield
            Pc, tPc = DT["Pa"]
            Pn, tPn = DT["Pb"]
            S.op("dve", lambda e, Pc=Pc: e.tensor_tensor(f2(Pc), f2(XT0), ident4_b, ALU.add), [tXT0, t_par], [tPc])
            Xc, tXc, XTc, tXTc = X0, tX0, XT0, tXT0
            nxt = [(DT["Xb"], DT["XTb"]), (DT["Xa"], DT["XTa"])]
            for j in range(1, 7):
                (Xn, tXn), (XTn, tXTn) = nxt[j % 2]
                if j < 6:
                    b = rec_bank()
                    for c4 in range(4):
                        S.op("pe", lambda e, b=b, c4=c4, Xc=Xc, XTc=XTc: e.matmul(PB[b][:, c4 * 128:(c4 + 1) * 128], Xc[:, c4, :],
                                                                                 XTc[:, c4, :], start=True, stop=True),
                             [tXc, tXTc], [tPB[b]])
                b2 = rec_bank()
                for c4 in range(4):
                    S.op("pe", lambda e, b2=b2, c4=c4, Xc=Xc, XTc=XTc: e.matmul(PB[b2][:, c4 * 128:(c4 + 1) * 128], XTc[:, c4, :],
                                                                               Xc[:, c4, :], start=True, stop=True),
                         [tXc, tXTc], [tPB[b2]])
                if j < 6:
                    S.op("act", lambda e, b=b, XTn=XTn: e.copy(f2(XTn), PB[b][:, :]), [tPB[b]], [tXTn])
                S.op("dve", lambda e, b2=b2, Xn=Xn: e.tensor_copy(f2(Xn), PB[b2][:, :]), [tPB[b2]], [tXn])
                yield
                b3 = rec_bank()
                for c4 in range(4):
                    S.op("pe", lambda e, b3=b3, c4=c4, Xn=Xn, Pc=Pc: e.matmul(PB[b3][:, c4 * 128:(c4 + 1) * 128], Xn[:, c4, :],
                                                                             Pc[:, c4, :], start=True, stop=True),
                         [tXn, tPc], [tPB[b3]])
                S.op("dve", lambda e, b3=b3, Pn=Pn, Pc=Pc: e.tensor_tensor(f2(Pn), PB[b3][:, :], f2(Pc), ALU.add),
                     [tPB[b3], tPc], [tPn])
                Pc, tPc, Pn, tPn = Pn, tPn, Pc, tPc
                Xc, tXc, XTc, tXTc = Xn, tXn, XTn, tXTn
                yield
            b = rec_bank()
            for c4 in range(4):
                S.op("pe", lambda e, b=b, c4=c4: e.matmul(PB[b][:, c4 * 64:(c4 + 1) * 64], AakT[:, c4, :], TOKM[:, c4, 2, hs],
                                                           start=True, stop=True), [tAakT, t_TOKM], [tPB[b]])
            S.op("act", lambda e, b=b: e.copy(G[:, :, h, 1, :], PB[b][:, 0:256].rearrange("p (c v) -> p c v", c=4)),
                 [tPB[b]], [t_G])
            yield
            b = rec_bank()
            for c4 in range(4):
                S.op("pe", lambda e, b=b, c4=c4, Pc=Pc: e.matmul(PB[b][:, c4 * 128:(c4 + 1) * 128], Pc[:, c4, :],
                                                                 G[:, c4, h, :, :].rearrange("p a k -> p (a k)"),
                                                                 start=True, stop=True), [tPc, t_G], [tPB[b]])
            S.op("act", lambda e, b=b: e.copy(Fh.rearrange("p c a k -> p (c a k)"), PB[b][:, :]), [tPB[b]], [tF])

        def rec_pair_finish(s, hp):
            si = s * 4 + hp
            bM, bN, bR, bY = rec_bank(), rec_bank(), rec_bank(), rec_bank()
            yield
            for h in range(2):
                hs = slice(h * 64, (h + 1) * 64)
                Fh, tF = HT[h]["F"]
                ArbT, tArbT = HT[h]["ArbT"]
                ArkT, tArkT = HT[h]["ArkT"]
                for c4 in range(4):
                    S.op("pe", lambda e, c4=c4, hs=hs, Fh=Fh: e.matmul(PB[bM][hs, c4 * 64:(c4 + 1) * 64], Fh[:, c4, 0, :],
                                                                       TOKM[:, c4, 0, hs], start=True, stop=True),
                         [tF, t_TOKM], [tPB[bM]])
                for c4 in range(4):
                    S.op("pe", lambda e, c4=c4, hs=hs, Fh=Fh: e.matmul(PB[bN][hs, c4 * 64:(c4 + 1) * 64], TOKM[:, c4, 0, hs],
                                                                       Fh[:, c4, 1, :], start=True, stop=False),
                         [tF, t_TOKM], [tPB[bN]])
                    S.op("pe", lambda e, c4=c4, hs=hs: e.matmul(PB[bN][hs, c4 * 64:(c4 + 1) * 64], TOKM[:, c4, 1, hs],
                                                                TOKM[:, c4, 2, hs], start=False, stop=True),
                         [t_TOKM], [tPB[bN]])
                for c4 in range(4):
                    S.op("pe", lambda e, c4=c4, hs=hs, Fh=Fh, ArbT=ArbT: e.matmul(PB[bR][hs, c4 * 128:(c4 + 1) * 128], Fh[:, c4, 0, :],
                                                                                  ArbT[:, c4, :], start=True, stop=True),
                         [tF, tArbT], [tPB[bR]])
                for c4 in range(4):
                    S.op("pe", lambda e, c4=c4, hs=hs, Fh=Fh, ArbT=ArbT: e.matmul(PB[bY][hs, c4 * 128:(c4 + 1) * 128], Fh[:, c4, 1, :],
                                                                                  ArbT[:, c4, :], start=True, stop=False),
                         [tF, tArbT], [tPB[bY]])
                    S.op("pe", lambda e, c4=c4, hs=hs, ArkT=ArkT: e.matmul(PB[bY][hs, c4 * 128:(c4 + 1) * 128], TOKM[:, c4, 2, hs],
                                                                           ArkT[:, c4, :], start=False, stop=True),
                         [t_TOKM, tArkT], [tPB[bY]])
            S.op("dve", lambda e: e.tensor_tensor(MTs.rearrange("p c k -> p (c k)"), PB[bM][:, 0:256], istack, ALU.add),
                 [tPB[bM], t_par], [t_MTs])
            for c4 in range(4):
                S.op("act", lambda e, c4=c4: e.activation(Npp[:, c4, :], PB[bN][:, c4 * 64:(c4 + 1) * 64], AF.Copy,
                                                          scale=E1[:, c4 * 128 + 127:c4 * 128 + 128]),
                     [tPB[bN], t_E1], [t_Npp])
            S.op("dve", lambda e: e.tensor_tensor(c4v(RhT), c4v(PB[bR][:, :]), AR[:, :, 1, :], ALU.add), [tPB[bR], t_AR], [t_RhT])
            S.op("act", lambda e: e.copy(Y0T, PB[bY][:, :]), [tPB[bY]], [t_Y0T])
            yield
            for c4 in range(4):
                S.op("act", lambda e, c4=c4: e.activation(Zb[:, si, :], Sm[:, si, :], AF.Copy, scale=din[:, c4:c4 + 1]),
                     [t_Sm[si], t_din], [t_Zb[si]])
                bz, by = rec_bank(), rec_bank()
                for h in range(2):
                    hs = slice(h * 64, (h + 1) * 64)
                    S.op("pe", lambda e, c4=c4, hs=hs, bz=bz: e.matmul(PB[bz][hs, 0:64], MTs[hs, c4, :], Zb[hs, si, :],
                                                                       start=True, stop=True), [t_MTs, t_Zb[si]], [tPB[bz]])
                    S.op("pe", lambda e, c4=c4, hs=hs, by=by: e.matmul(PB[by][hs, 0:128], Zb[hs, si, :],
                                                                       RhT[hs, c4 * 128:(c4 + 1) * 128], start=True, stop=True),
                         [t_RhT, t_Zb[si]], [tPB[by]])
                S.op("dve", lambda e, c4=c4, bz=bz: e.scalar_tensor_tensor(Sm[:, si, :], PB[bz][:, 0:64],
                                                                             E1[:, c4 * 128 + 127:c4 * 128 + 128], Npp[:, c4, :],
                                                                             ALU.mult, ALU.add),
                     [tPB[bz], t_E1, t_Npp], [t_Sm[si]])
                S.op("dve", lambda e, c4=c4, by=by: e.tensor_tensor(yT[:, c4 * 128:(c4 + 1) * 128], PB[by][:, 0:128],
                                                                     Y0T[:, c4 * 128:(c4 + 1) * 128], ALU.add),
                     [tPB[by], t_Y0T], [t_yT])
                yield

        def rwkv_pair_post(s, hp):
            S.op("act", lambda e: e.copy(ybf, yT), [t_yT], [t_ybf])
            S.op("act", lambda e: e.activation(ysq, yT, AF.Square), [t_yT], [t_ysq])
            S.op("pe", lambda e: e.matmul(PB[PS][:, :], bones_b, ybf, start=True, stop=True), [t_par, t_ybf], [tPB[PS]])
            S.op("dve", lambda e: e.tensor_scalar(gmu, PB[PS][:, :], 1.0 / 64, None, ALU.mult), [tPB[PS]], [t_gmu])
            S.op("pe", lambda e: e.matmul(PB[PS][:, :], bones_b, ysq, start=True, stop=True), [t_par, t_ysq], [tPB[PS]])
            S.op("dve", lambda e: e.tensor_tensor(gtmp, gmu, gmu, ALU.mult), [t_gmu], [t_gtmp])
            S.op("dve", lambda e: e.scalar_tensor_tensor(gvar, PB[PS][:, :], 1.0 / 64, gtmp, ALU.mult, ALU.subtract),
                 [tPB[PS], t_gtmp], [t_gvar])
            yield
            S.op("dve", lambda e: e.tensor_scalar(gvar, gvar, GN_EPS, None, ALU.add), [t_gvar], [t_gvar])
            S.op("act", lambda e: e.activation(gvar, gvar, AF.Ln), [t_gvar], [t_gvar])
            S.op("act", lambda e: e.activation(gvar, gvar, AF.Exp, scale=-0.5), [t_gvar], [t_gvar])
            S.op("dve", lambda e: e.tensor_tensor(gtmp, yT, gmu, ALU.subtract), [t_yT, t_gmu], [t_gtmp])
            S.op("dve", lambda e: e.tensor_tensor(gtmp, gtmp, gvar, ALU.mult), [t_gtmp, t_gvar], [t_gtmp])
            S.op("act", lambda e: e.activation(gtmp, gtmp, AF.Identity, scale=col(PC_LXG, hp), bias=col(PC_LXB, hp)),
                 [t_gtmp, t_par], [t_gtmp])
            S.op("dve", lambda e: e.tensor_tensor(gtmp, gtmp, bonus, ALU.add), [t_gtmp, t_bonus], [t_gtmp])
            S.op("pe", lambda e: e.matmul(PB[PS][:, :], gate_sb[:, hp * 128:(hp + 1) * 128], sg, start=True, stop=True),
                 [t_parA, t_sg], [tPB[PS]])
            S.op("dve", lambda e: e.tensor_tensor(mixT[:, hp, :], PB[PS][:, :], gtmp, ALU.mult), [t_gtmp, tPB[PS]], [t_mix[hp]])
            yield

        return dict(prep=rwkv_pair_prep, tail=prep_tail, head=rec_head, finish=rec_pair_finish, post=rwkv_pair_post)

    PAIR = [make_pair(0), make_pair(1)]

    conv_flag = [False]

    def conv_branch(s):
        for q in range(4):
            hi = s * 4 + q
            ba = inproj_chunk(14 + q)
            yield
            bg = inproj_chunk(18 + q)
            yield
            S.op("act", lambda e, bg=bg: e.activation(sgate, PB[bg][:, :], AF.Sigmoid), [tPB[bg]], [t_sgate])
            yield
            S.op("act", lambda e, hi=hi, q=q: e.copy(hglu[:, q, 0:30], hhist[:, hi, :]), [t_hhist[hi]], [t_hglu[q]])
            S.op("dve", lambda e, q=q, ba=ba: e.tensor_tensor(hglu[:, q, 30:30 + BLK], PB[ba][:, :], sgate, ALU.mult),
                 [tPB[ba], t_sgate], [t_hglu[q]])
            S.op("act", lambda e, hi=hi, q=q: e.copy(hhist[:, hi, :], hglu[:, q, BLK:BLK + 30]), [t_hglu[q]], [t_hhist[hi]])
            yield
        conv_flag[0] = True
        for q in range(4):
            b = big_bank()
            for j in range(CW):
                for g in range(2):
                    gs = slice(g * 64, (g + 1) * 64)
                    S.op("pe", lambda e, b=b, j=j, q=q, gs=gs: e.matmul(PB[b][gs, :], diagw[gs, q * CW + j, :],
                                                                       hglu[gs, q, j:j + BLK],
                                                                       start=(j == 0), stop=(j == CW - 1)),
                         [t_cst, t_hglu[q]], [tPB[b]])
                if j % 8 == 7:
                    yield
            S.op("act", lambda e, b=b, q=q: e.activation(hcv[q], PB[b][:, :], AF.Identity, bias=col(PC_CB, q)),
                 [tPB[b], t_par], [t_hcv[q]])
            yield
            S.op("act", lambda e, b=b, q=q: e.activation(hsq[q], PB[b][:, :], AF.Square, bias=col(PC_CB, q)),
                 [tPB[b], t_par], [t_hsq[q]])
            yield
            S.op("dve", lambda e, q=q: e.tensor_copy(hcb[q], hcv[q]), [t_hcv[q]], [t_hcb[q]])
            yield
        for q in range(4):
            S.op("pe", lambda e, q=q: e.matmul(PB[PS][:, :], aones_b, hcb[q], start=(q == 0), stop=(q == 3)),
                 [t_par, t_hcb[q]], [tPB[PS]])
        S.op("dve", lambda e: e.tensor_scalar(cmu, PB[PS][:, :], 1.0 / 512, None, ALU.mult), [tPB[PS]], [t_cmu])
        for q in range(4):
            S.op("pe", lambda e, q=q: e.matmul(PB[PS][:, :], aones_b, hsq[q], start=(q == 0), stop=(q == 3)),
                 [t_par, t_hsq[q]], [tPB[PS]])
        S.op("dve", lambda e: e.tensor_tensor(ctmp, cmu, cmu, ALU.mult), [t_cmu], [t_ctmp])
        S.op("dve", lambda e: e.scalar_tensor_tensor(cvar, PB[PS][:, :], 1.0 / 512, ctmp, ALU.mult, ALU.subtract),
             [tPB[PS], t_ctmp], [t_cvar])
        yield
        S.op("dve", lambda e: e.tensor_scalar(cvar, cvar, LN_EPS, None, ALU.add), [t_cvar], [t_cvar])
        yield
        S.op("act", lambda e: e.activation(cvar, cvar, AF.Ln), [t_cvar], [t_cvar])
        yield
        S.op("act", lambda e: e.activation(cvar, cvar, AF.Exp, scale=-0.5), [t_cvar], [t_cvar])
        yield
        for q in range(4):
            S.op("dve", lambda e, q=q: e.tensor_tensor(ctmp, hcv[q], cmu, ALU.subtract), [t_hcv[q], t_cmu], [t_ctmp])
            yield
            S.op("dve", lambda e, q=q: e.tensor_tensor(ctmp, ctmp, cvar, ALU.mult), [t_ctmp, t_cvar], [t_ctmp])
            yield
            S.op("act", lambda e, q=q: e.activation(mixT[:, 4 + q, :], ctmp, AF.Silu, scale=col(PC_CLG, q), bias=col(PC_CLB, q)),
                 [t_ctmp, t_par], [t_mix[4 + q]])
            yield

    def out_proj(s, blk):
        tok0 = s * SEQ + blk * BLK
        for j in range(4):
            sl = xs_n[0] % 2
            xs_n[0] += 1
            r0 = tok0 + j * 128
            dma("pool", XS[sl], x_d[r0:r0 + 128, :], [], [tXS[sl]], "xsp%d" % sl)
            yield
            for half in range(2):
                b = big_bank()
                for k in range(8):
                    S.op("pe", lambda e, b=b, k=k, j=j, half=half: e.matmul(PB[b][:, :], mixT[:, k, j * 128:(j + 1) * 128],
                                                                           w_out_sb[:, k, half * 512:(half + 1) * 512],
                                                                           start=(k == 0), stop=(k == 7)),
                         [t_w_out] + t_mix, [tPB[b]])
                S.op("dve", lambda e, b=b, sl=sl, half=half: e.tensor_tensor(x1t[:, half * 512:(half + 1) * 512], PB[b][:, :],
                                                                              XS[sl][:, half * 512:(half + 1) * 512], ALU.add),
                     [tPB[b], tXS[sl]], [t_x1t])
                yield
            dma("pool", x1_d[r0:r0 + 128, :], x1t, [t_x1t], [], "x1t")
            yield

    def drive(gens):
        gens = list(gens)
        while gens:
            for g in list(gens):
                try:
                    next(g)
                except StopIteration:
                    gens.remove(g)

    ONE = bool(os.environ.get("DEV_ONE"))

    def run_all(g):
        if g is None:
            return
        for _ in g:
            pass

    def R_gen(s, hp):
        fn = PAIR[hp % 2]
        gens = [fn["head"](s, hp, 0), fn["head"](s, hp, 1)]
        while gens:
            for g in list(gens):
                try:
                    next(g)
                except StopIteration:
                    gens.remove(g)
            yield
        yield from fn["finish"](s, hp)
        yield from fn["post"](s, hp)

    def lora_prep0_gen(s):
        b = inproj_chunk(12)
        shift_evac(b, 12, s, p12, t_p12)
        yield
        S.op("act", lambda e: e.activation(lw[0:64, :], p12[0:64, :], AF.Tanh), [t_p12], [t_lw])
        yield
        S.op("act", lambda e: e.copy(lw[64:128, :], p12[64:128, :]), [t_p12], [t_lw])
        yield
        b = inproj_chunk(13)
        shift_evac(b, 13, s, p12, t_p12)
        yield
        S.op("act", lambda e: e.activation(sg, p12, AF.Sigmoid), [t_p12], [t_sg])
        yield
        yield from PAIR[0]["prep"](s, 0)

    prefetched = [False]
    pre_done = [False]
    order = [(blk, s) for blk in range(NBLK) for s in range(NSEQ)]
    if ONE:
        order = [(0, 0)]
    if stage < 1:
        order = []
    for oi, (blk, s) in enumerate(order):
        tok0 = s * SEQ + blk * BLK
        if not prefetched[0]:
            run_all(front_gen(tok0, gA, t_parA, x_d))
        prefetched[0] = False
        if not pre_done[0]:
            run_all(lora_prep0_gen(s))
        pre_done[0] = False
        run_all(PAIR[0]["tail"]())
        nxt = order[oi + 1] if oi + 1 < len(order) else None
        for hp in range(4):
            gens = []
            if stage >= 2:
                gens.append(R_gen(s, hp))
            if hp < 3:
                gens.append(PAIR[(hp + 1) % 2]["prep"](s, hp + 1))
            elif stage >= 3:
                conv_flag[0] = False
                gens.append(conv_branch(s))
                if nxt is not None:
                    gens.append(front_gen(nxt[1] * SEQ + nxt[0] * BLK, gA, t_parA, x_d, wait_flag=conv_flag))
                    prefetched[0] = True
            drive(gens)
            if hp < 3:
                run_all(PAIR[(hp + 1) % 2]["tail"]())
        if stage >= 4:
            gens = [out_proj(s, blk)]
            if nxt is not None and prefetched[0]:
                gens.append(lora_prep0_gen(nxt[1]))
                pre_done[0] = True
            drive(gens)

    if debug_cols:
        DEBUG_HOOK(locals())
        build_program.dbg_names = dbg_names

    if stage >= 5:
        barrier()
        top[0] = persist_top
        BB = 256
        w1_sb = sb("w1_sb", [128, 8, DFF], BF16)
        w2_sb = sb("w2_sb", [128, 32, D], BF16)
        gB = sb("gB", [128, 2, D])
        f1T = sb("f1T", [128, 32, BB], BF16)
        t_f1 = [T("f1T%d" % j) for j in range(32)]
        rl, t_rl = W("rl", (128, BB))
        xk = [sb("xk%d" % j, [128, D]) for j in range(2)]
        t_xk = [T("xk%d" % j) for j in range(2)]
        ot = [sb("ot%d" % j, [128, D]) for j in range(2)]
        t_ot = [T("ot%d" % j) for j in range(2)]
        x2, t_x2 = W("x2", (128, D))
        print("phase B SBUF top:", top[0])
        t_w1 = [T("w1_%d" % k) for k in range(8)]
        t_w2 = [T("w2_%d" % k) for k in range(8)]
        t_parB = T("paramsB")
        dma("sp", gB[:, 0, :], gvec_d[1:2, :].broadcast_to([128, D]), [], [t_parB], "initB")
        dma("sp", gB[:, 1, :], gvec_d[2:3, :].broadcast_to([128, D]), [], [t_parB], "initB")
        w1_v = w_ff1_d.rearrange("(k p) n -> p k n", p=128)
        t_w1g = [T("w1g%d" % g) for g in range(8)]
        for g in range(8):
            dma("pool", w1_sb[:, :, g * 512:(g + 1) * 512], w1_v[:, :, g * 512:(g + 1) * 512], [], [t_w1g[g]], "w1g%d" % g,
                max_dma_last_dim=4096)
        w2_v = w_ff2_d.rearrange("(j p) n -> p j n", p=128)
        for k in range(8):
            dma("pool", w2_sb[:, k * 4:(k + 1) * 4, :], w2_v[:, k * 4:(k + 1) * 4, :], [], [t_w2[k]], "all:w2",
                max_dma_last_dim=4096)
        on = [0]
        rl2, t_rl2 = W("rl2", (128, BB))
        RL = [(rl, t_rl), (rl2, t_rl2)]
        F1RING = (0, 1, 3, 4)
        F2RING = (5, 6, 7)
        f1n = [0]
        f2n = [0]
        NB = TOK // BB

        def front(tb):
            par = tb % 2
            load_norm_transpose(tb * BB, gB[:, 0, :], t_parB, x1_d, 2, hoff=par * BB, tbase=par * 2)

        front(0)
        for tb in range(NB):
            tok0 = tb * BB
            par = tb % 2
            for j in range(32):
                b = F1RING[f1n[0] % 4]
                f1n[0] += 1
                for k in range(8):
                    S.op("pe", lambda e, b=b, k=k, j=j, par=par: e.matmul(PB[b][:, 0:BB], w1_sb[:, k, j * 128:(j + 1) * 128],
                                                                           hT[:, k, par * BB:(par + 1) * BB],
                                                                           start=(k == 0), stop=(k == 7)),
                         [t_w1g[j // 4], t_hT[par * 2], t_hT[par * 2 + 1]], [tPB[b]])
                rlj, t_rlj = RL[j % 2]
                S.op("act", lambda e, b=b, rlj=rlj: e.activation(rlj, PB[b][:, 0:BB], AF.Relu), [tPB[b]], [t_rlj])
                S.op("dve", lambda e, j=j, rlj=rlj: e.tensor_tensor(f1T[:, j, :], rlj, rlj, ALU.mult), [t_rlj], [t_f1[j]])
            if tb + 1 < NB:
                front(tb + 1)
            for jt in range(2):
                r0 = tok0 + jt * 128
                dma("sp", xk[jt], x1_d[r0:r0 + 128, :], [], [t_xk[jt]], "xk%d" % jt)
                for half in range(2):
                    b = F2RING[f2n[0] % 3]
                    f2n[0] += 1
                    for j in range(32):
                        S.op("pe", lambda e, b=b, j=j, jt=jt, half=half: e.matmul(PB[b][:, :], f1T[:, j, jt * 128:(jt + 1) * 128],
                                                                                 w2_sb[:, j, half * 512:(half + 1) * 512],
                                                                                 start=(j == 0), stop=(j == 31)),
                             [t_w2[j // 4], t_f1[j]], [tPB[b]])
                    S.op("dve", lambda e, b=b, jt=jt, half=half: e.tensor_tensor(x2[:, half * 512:(half + 1) * 512], PB[b][:, :],
                                                                                  xk[jt][:, half * 512:(half + 1) * 512], ALU.add),
                         [tPB[b], t_xk[jt]], [t_x2])
                osl = on[0] % 2
                on[0] += 1
                cj = ss4[:, 4 + jt:5 + jt]
                S.op("act", lambda e, osl=osl, cj=cj: e.activation(ot[osl], x2, AF.Square, accum_out=cj),
                     [t_x2], [t_ot[osl], t_ss[4 + jt]])
                rms_rstd(cj, t_ss[4 + jt], D, RMS_EPS)
                S.op("dve", lambda e, osl=osl, cj=cj: e.scalar_tensor_tensor(ot[osl], x2, cj, gB[:, 1, :], ALU.mult, ALU.mult),
                     [t_x2, t_ss[4 + jt], t_parB], [t_ot[osl]])
                dma("sp", out_d[r0:r0 + 128, :], ot[osl], [t_ot[osl]], [], "ot%d" % osl)

    S.finalize()

    sems = {}
    keys = set()
    for o in S.ops:
        if o.dkey is not None:
            keys.add(("dma", o.dkey))
    for e_ in Sched.ENGS:
        keys.add(("eng", e_))
    for k in sorted(keys):
        sems[k] = es.enter_context(nc.semaphore("s_%s_%s" % (k[0], k[1].replace(":", "_"))))
    dfinal = [(("dma", k), 16 * c) for k, c in S.dcount.items()]
    print("ops:", {e_: len(S.per[e_]) for e_ in Sched.ENGS}, "sems:", len(sems))
    block = es.enter_context(nc.Block())

    @block.sync
    def _(e):
        S.emit("sp", e, sems)
        for k, v in dfinal:
            e.wait_ge(sems[k], v)

    @block.scalar
    def _(e):
        S.emit("act", e, sems)

    @block.vector
    def _(e):
        S.emit("dve", e, sems)

    @block.gpsimd
    def _(e):
        S.emit("pool", e, sems)

    @block.tensor
    def _(e):
        S.emit("pe", e, sems)

    es.close()
    return nc


def DEBUG_HOOK(L):
    pass


def host_constants():
    eye = np.eye(128, dtype=np.float32)
    bo = np.zeros((128, 128), np.float32)
    bo[:64, :64] = 1.0
    bo[64:, 64:] = 1.0
    i = np.arange(128)[:, None]
    t = np.arange(128)[None, :]
    strictT = (t > i).astype(np.float32)
    inclT = (t >= i).astype(np.float32)
    strictL = (t < i).astype(np.float32)
    ist = (np.arange(128)[:, None] % 64 == np.arange(64)[None, :]).astype(np.float32)
    parts = [eye, bo, np.ones((128, 128), np.float32), eye, eye, eye, eye,
             strictT, inclT, strictT, inclT, strictL, strictL, strictL, strictL,
             ist, ist, ist, ist, np.ones((128, 512), np.float32)]
    cst = np.concatenate(parts, axis=1)
    assert cst.shape == (128, NCSTB), cst.shape
    return np.ascontiguousarray(cst)


def host_layout(inp):
    f = lambda a: np.ascontiguousarray(np.asarray(a, dtype=np.float32))
    colv = lambda v: f(v).reshape(-1, 128).T
    pcol = np.concatenate([
        colv(inp["shift_mu"][0]), colv(inp["w0"][0]), colv(inp["a0"][0]), colv(inp["k_k"][0]), colv(inp["k_a"][0]),
        colv(inp["r_k"][0]), colv(inp["ln_x_g"][0]), colv(inp["ln_x_b"][0]), colv(inp["conv_b"][0]),
        colv(inp["conv_ln_g"][0]), colv(inp["conv_ln_b"][0])], axis=1)
    assert pcol.shape == (128, NPCOL)
    cwt = f(inp["conv_w"][0]).T.reshape(4, 128, CW).transpose(1, 0, 2).reshape(128, 4 * CW)
    shared = {
        "w_in": f(inp["w_in"][0]), "w_out": f(inp["w_out"][0]), "w_ff1": f(inp["w_ff1"][0]), "w_ff2": f(inp["w_ff2"][0]),
        "lora_up": np.concatenate([f(inp["w_decay_up"][0]), f(inp["w_aaa_up"][0])], axis=0),
        "w_gate_up": f(inp["w_gate_up"][0]),
        "pcol": np.ascontiguousarray(pcol), "cw": np.ascontiguousarray(cwt),
        "gvec": np.stack([f(inp["norm_mix_g"][0]), f(inp["norm_mlp_g"][0]), f(inp["norm_final_g"])], axis=0),
        "cst": host_constants(),
    }
    x = f(inp["x"]).reshape(NCORES, TOK, D)
    return [dict(shared, x=x[c]) for c in range(NCORES)]


def kernel(**inputs):
    nc = build_program()
    in_maps = host_layout(inputs)
    res = run_bass_kernel_spmd(nc, in_maps, core_ids=list(range(NCORES)))
    out = np.stack([np.asarray(r["out"], dtype=np.float32) for r in res.results], axis=0)
    return out.reshape(16, SEQ, D)
```

```python
import contextlib
import numpy as np
import concourse.bass as bass
import concourse.mybir as mybir
from concourse.bass_utils import run_bass_kernel_spmd

F32 = mybir.dt.float32
BF16 = mybir.dt.bfloat16
AF = mybir.ActivationFunctionType
ALU = mybir.AluOpType

NCORES = 8
D = 1024
SEQ = 2048
NSEQ = 2
TOK = NSEQ * SEQ
BLK = 512
NBLK = SEQ // BLK
DIN = 2816
DFF = 4096
CW = 31
C0 = float(np.exp(-0.5))
RMS_EPS = 1e-6
GN_EPS = 64e-5
LN_EPS = 1e-5

NCSTB = 2688
PC_MU, PC_W0, PC_A0, PC_KK, PC_KA, PC_RK, PC_LXG, PC_LXB, PC_CB, PC_CLG, PC_CLB = 0, 14, 18, 22, 26, 30, 34, 38, 42, 46, 50
NPCOL = 54


class T:
    __slots__ = ("name", "w", "r")

    def __init__(self, name):
        self.name = name
        self.w = None
        self.r = []


class Op:
    __slots__ = ("eng", "fn", "deps", "pos", "ticket", "signal", "dkey", "dseq", "waits")


class Sched:
    ENGS = ("pe", "act", "dve", "pool", "sp")

    def __init__(self):
        self.per = {e: [] for e in self.ENGS}
        self.ops = []
        self.dcount = {}

    def op(self, eng, fn, reads=(), writes=(), dkey=None):
        o = Op()
        o.eng = eng
        o.fn = fn
        o.signal = False
        o.dkey = dkey
        o.dseq = 0
        o.ticket = 0
        deps = set()
        for t in reads:
            if t.w is not None:
                deps.add(t.w)
        for t in writes:
            if t.w is not None:
                deps.add(t.w)
            deps.update(t.r)
        for t in reads:
            t.r.append(o)
        for t in writes:
            t.w = o
            t.r = []
        deps.discard(o)
        if dkey is not None and dkey.startswith(("init", "all:")):
            deps = {d for d in deps if d.dkey != dkey}
        o.deps = deps
        o.pos = len(self.per[eng])
        self.per[eng].append(o)
        self.ops.append(o)
        if dkey is not None:
            c = self.dcount.get(dkey, 0) + 1
            self.dcount[dkey] = c
            o.dseq = c
        return o

    @staticmethod
    def _skip(d, o):
        return d.dkey is None and o.dkey is None and d.eng == "pe" and o.eng == "pe"

    def finalize(self):
        for o in self.ops:
            for d in o.deps:
                if d.dkey is None and not self._skip(d, o):
                    d.signal = True
        for e in self.ENGS:
            n = 0
            for o in self.per[e]:
                if o.dkey is None and o.signal:
                    n += 1
                    o.ticket = n
        for e in self.ENGS:
            known = {}
            for o in self.per[e]:
                w = {}
                for d in o.deps:
                    if d.dkey is not None:
                        key = ("dma", d.dkey)
                        val = 16 * (self.dcount[d.dkey] if d.dkey.startswith(("init", "all:")) else d.dseq)
                    else:
                        if self._skip(d, o):
                            continue
                        key = ("eng", d.eng)
                        val = d.ticket
                    if known.get(key, 0) >= val:
                        continue
                    if w.get(key, 0) < val:
                        w[key] = val
                for k, v in w.items():
                    known[k] = v
                o.waits = list(w.items())

    def emit(self, eng, e, sems):
        for o in self.per[eng]:
            for key, val in o.waits:
                e.wait_ge(sems[key], val)
            if o.fn is None:
                continue
            inst = o.fn(e)
            if o.dkey is not None:
                inst.then_inc(sems[("dma", o.dkey)], 16)
            elif o.signal:
                inst.then_inc(sems[("eng", o.eng)], 1)


def build_program(stage=99, debug_cols=0):
    nc = bass.Bass("TRN2", target_bir_lowering=False)
    S = Sched()
    es = contextlib.ExitStack()

    def dram(name, shape, dt=F32, kind="ExternalInput"):
        return nc.dram_tensor(name, list(shape), dt, kind=kind).ap()

    x_d = dram("x", [TOK, D])
    w_in_d = dram("w_in", [D, DIN])
    w_out_d = dram("w_out", [D, D])
    w_ff1_d = dram("w_ff1", [D, DFF])
    w_ff2_d = dram("w_ff2", [DFF, D])
    lora_d = dram("lora_up", [128, 512])
    gate_d = dram("w_gate_up", [128, 512])
    pcol_d = dram("pcol", [128, NPCOL])
    cw_d = dram("cw", [128, 4 * CW])
    gvec_d = dram("gvec", [3, D])
    cst_d = dram("cst", [128, NCSTB])
    out_d = dram("out", [TOK, D], kind="ExternalOutput")
    x1_d = dram("x1_scratch", [TOK, D], kind="Internal")
    dbg_d = dram("dbg", [128, max(debug_cols, 8)], kind="ExternalOutput") if debug_cols else None
    dbg_off = [0]

    ARENA = 212000
    big = es.enter_context(nc.sbuf_tensor("arena", [128, ARENA // 2], BF16))
    top = [0]

    def sb(name, shape, dt=F32):
        n = 1
        for v in shape[1:]:
            n *= v
        nbytes = n * (4 if dt == F32 else 2)
        off = (top[0] + 63) // 64 * 64
        assert off + nbytes <= ARENA, ("SBUF arena overflow", name, off + nbytes)
        top[0] = off + nbytes
        v = big[:, off // 2:(off + nbytes) // 2]
        if dt == F32:
            v = v.bitcast(F32)
        if len(shape) == 3:
            v = v.rearrange("p (a b) -> p a b", a=shape[1])
        elif len(shape) == 4:
            v = v.rearrange("p (a b c) -> p a b c", a=shape[1], b=shape[2])
        elif len(shape) == 5:
            v = v.rearrange("p (a b c d) -> p a b c d", a=shape[1], b=shape[2], c=shape[3])
        return v

    def W(name, shape=(128, BLK), dt=F32):
        return sb(name, list(shape), dt), T(name)

    import os
    SKD = os.environ.get("SKIPD", "").split(",")
    dn = [0]
    def dma(eng, out, in_, reads, writes, dkey, **kw):
        if dkey in SKD or (eng + ":" + dkey) in SKD:
            return
        dn[0] += 1
        if str(dn[0]) in SKD:
            return
        if dkey.startswith(("init", "all:")):
            dkey = dkey + "_" + eng
        S.op(eng, lambda e: e.dma_start(out=out, in_=in_, **kw), reads=reads, writes=writes, dkey=dkey)

    dbg_names = {}

    def dump(name, ap_sb, tt, ncols, bf=False):
        o = dbg_off[0]
        dbg_off[0] += ncols
        dbg_names[name] = (o, ncols)
        dma("pool" if bf else "sp", dbg_d[:, o:o + ncols], ap_sb, [tt], [], "all:dbg")

    def barrier():
        deps = set()
        lastd = {}
        for o in S.ops:
            if o.dkey is not None:
                lastd[o.dkey] = o
        deps.update(lastd.values())
        for e_ in Sched.ENGS:
            for o in reversed(S.per[e_]):
                if o.dkey is None and o.fn is not None:
                    deps.add(o)
                    break
        for e_ in Sched.ENGS:
            o = S.op(e_, None)
            o.deps = set(deps)

    PB = {}
    tPB = {}
    for i in (0, 1, 3, 4, 5, 6, 7):
        PB[i] = es.enter_context(nc.psum_tensor("pb%d" % i, [128, 512], F32))
        tPB[i] = T("pb%d" % i)
    PTB = es.enter_context(nc.psum_tensor("ptb", [128, 1024], BF16))
    tPTB = T("ptb")
    big_i = [0]

    def big_bank():
        big_i[0] ^= 1
        return big_i[0]

    rec_i = [0]

    REC_RING = (4, 5, 6, 7)

    def rec_bank():
        rec_i[0] = (rec_i[0] + 1) % len(REC_RING)
        return REC_RING[rec_i[0]]

    cstb = sb("cstb", [128, NCSTB], BF16)
    pcol = sb("pcol", [128, NPCOL])
    omm = sb("omm", [128, 14])
    cw_sb = sb("cw_sb", [128, 4 * CW])
    XS = [sb("xs%d" % i, [128, D]) for i in range(2)]
    tXS = [T("xs%d" % i) for i in range(2)]
    hb = [sb("hb%d" % i, [128, D], BF16) for i in range(2)]
    t_hb = [T("hb%d" % i) for i in range(2)]
    ss4 = sb("ss4", [128, 8])
    t_ss = [T("ss%d" % i) for i in range(8)]
    hT = sb("hT", [128, 8, BLK], BF16)
    t_hT = [T("hT%d" % j) for j in range(4)]
    t_par = T("params")
    t_cst = T("cst_derived")
    PERSIST_TOP = None

    ident_b = cstb[:, 0:128]
    bones_b = cstb[:, 128:256]
    aones_b = cstb[:, 256:384]
    ident4_b = cstb[:, 384:896]
    maskT = cstb[:, 896:1408]
    maskL = cstb[:, 1408:1920]
    istack = cstb[:, 1920:2176]
    ones_b = cstb[:, 2176:2688]

    dma("sp", pcol, pcol_d, [], [t_par], "init")
    dma("sp", cw_sb, cw_d, [], [t_par], "init")
    dma("pool", cstb, cst_d, [], [t_par], "init", max_dma_last_dim=4096)
    S.op("dve", lambda e: e.tensor_scalar(omm, pcol[:, PC_MU:PC_MU + 14], -1.0, 1.0, ALU.mult, ALU.add),
         [t_par], [t_cst])

    xs_n = [0]

    def rms_rstd(cj, t_col, n, eps):
        S.op("dve", lambda e: e.tensor_scalar(cj, cj, 1.0 / n, eps, ALU.mult, ALU.add), [t_col], [t_col])
        S.op("act", lambda e: e.activation(cj, cj, AF.Ln), [t_col], [t_col])
        S.op("act", lambda e: e.activation(cj, cj, AF.Exp, scale=-0.5), [t_col], [t_col])

    def load_norm_transpose(tok0, gvec_ap, t_g, src_d, ntiles, keep=None, hoff=0, tbase=0):
        for j in range(ntiles):
            r0 = tok0 + j * 128
            if keep is None:
                sl = xs_n[0] % 2
                xs_n[0] += 1
                xt, txt, key = XS[sl], tXS[sl], "xs%d" % sl
            else:
                xt, txt, key = keep[0][j], keep[1][j], keep[2] + str(j)
            dma("sp", xt, src_d[r0:r0 + 128, :], [], [txt], key)
            cj = ss4[:, j:j + 1]
            sl2 = j % 2
            S.op("act", lambda e, xt=xt, cj=cj, sl2=sl2: e.activation(hb[sl2], xt, AF.Square, accum_out=cj),
                 [txt], [t_hb[sl2], t_ss[j]])
            rms_rstd(cj, t_ss[j], D, RMS_EPS)
            S.op("dve", lambda e, xt=xt, cj=cj, sl2=sl2: e.scalar_tensor_tensor(hb[sl2], xt, cj, gvec_ap, ALU.mult, ALU.mult),
                 [txt, t_ss[j], t_g], [t_hb[sl2]])
            for k in range(8):
                S.op("pe", lambda e, sl2=sl2, k=k: e.transpose(PTB[:, k * 128:(k + 1) * 128],
                                                               hb[sl2][:, k * 128:(k + 1) * 128], ident_b),
                     [t_hb[sl2], t_par], [tPTB])
            S.op("act", lambda e, j=j: e.copy(hT[:, :, hoff + j * 128:hoff + (j + 1) * 128],
                                              PTB[:, :].rearrange("p (k t) -> p k t", k=8)),
                 [tPTB], [t_hT[tbase + j]])


    def front_gen(tok0, gvec_ap, t_g, src_d, wait_flag=None):
        while wait_flag is not None and not wait_flag[0]:
            yield
        tiles = []
        for j in range(5):
            if j < 4:
                r0 = tok0 + j * 128
                sl = xs_n[0] % 2
                xs_n[0] += 1
                xt, txt = XS[sl], tXS[sl]
                dma("sp", xt, src_d[r0:r0 + 128, :], [], [txt], "xs%d" % sl)
                cj = ss4[:, j:j + 1]
                sl2 = j % 2
                S.op("act", lambda e, xt=xt, cj=cj, sl2=sl2: e.activation(hb[sl2], xt, AF.Square, accum_out=cj),
                     [txt], [t_hb[sl2], t_ss[j]])
                yield
                rms_rstd(cj, t_ss[j], D, RMS_EPS)
                yield
                S.op("dve", lambda e, xt=xt, cj=cj, sl2=sl2: e.scalar_tensor_tensor(hb[sl2], xt, cj, gvec_ap, ALU.mult, ALU.mult),
                     [txt, t_ss[j], t_g], [t_hb[sl2]])
                yield
            if j > 0:
                jj = j - 1
                sl2 = jj % 2
                for k in range(8):
                    S.op("pe", lambda e, sl2=sl2, k=k: e.transpose(PTB[:, k * 128:(k + 1) * 128],
                                                                   hb[sl2][:, k * 128:(k + 1) * 128], ident_b),
                         [t_hb[sl2], t_par], [tPTB])
                S.op("act", lambda e, jj=jj: e.copy(hT[:, :, jj * 128:(jj + 1) * 128],
                                                    PTB[:, :].rearrange("p (k t) -> p k t", k=8)),
                     [tPTB], [t_hT[jj]])
            yield

    persist_top = top[0]

    w_in_sb = sb("w_in_sb", [128, 8, DIN], BF16)
    w_out_sb = sb("w_out_sb", [128, 8, D], BF16)
    lora_sb = sb("lora_sb", [128, 512], BF16)
    gate_sb = sb("gate_sb", [128, 512], BF16)
    diagw = sb("diagw", [128, 4 * CW, 64], BF16)
    gA = sb("gA", [128, D])
    t_w_in = [T("w_in%d" % k) for k in range(8)]
    t_w_out = T("w_out")
    t_parA = T("paramsA")
    dma("sp", gA, gvec_d[0:1, :].broadcast_to([128, D]), [], [t_parA], "initA")
    dma("pool", lora_sb, lora_d, [], [t_parA], "initA")
    dma("pool", gate_sb, gate_d, [], [t_parA], "initA")
    w_in_v = w_in_d.rearrange("(k p) n -> p k n", p=128)
    import os
    SK = os.environ.get("SKIP", "")
    WPARTS = ((1536, 1792), (0, 1536), (1792, DIN))
    t_w_in_p = [T("w_in_part%d" % i) for i in range(3)]
    for pi, (c0, c1) in enumerate(WPARTS):
        for k in range(8):
            dma("pool", w_in_sb[:, k, c0:c1], w_in_v[:, k, c0:c1], [], [t_w_in_p[pi]], "all:w_inP%d" % pi, max_dma_last_dim=4096)

    def w_in_T(cc):
        c = cc * 128
        return t_w_in_p[0] if 1536 <= c < 1792 else (t_w_in_p[1] if c < 1536 else t_w_in_p[2])
    w_out_v = w_out_d.rearrange("(k p) n -> p k n", p=128)
    for k in range(8):
        if "wout" in SK:
            break
        dma("pool", w_out_sb[:, k, :], w_out_v[:, k, :], [], [t_w_out], "all:w_out", max_dma_last_dim=4096)
    t_diag = T("diagw")
    for i in range(4 * CW):
        if "diag" in SK:
            break
        S.op("pool", lambda e, i=i: e.tensor_scalar(diagw[:, i, :], istack[:, 0:64], cw_sb[:, i:i + 1], 1.0, ALU.mult, ALU.mult),
             [t_par], [t_diag])

    tmpb, t_tmpb = W("tmpb", (128, BLK + 1))
    carry = sb("carry", [128, NSEQ * 14])
    t_carry = [T("carry%d" % i) for i in range(NSEQ * 14)]
    S.op("dve", lambda e: e.memset(carry, 0.0), [], t_carry)

    lw, t_lw = W("lw", dt=BF16)
    sg, t_sg = W("sg", dt=BF16)
    pr, t_pr = W("pr")
    pk, t_pk = W("pk")
    pv, t_pv = W("pv")
    mixT = sb("mixT", [128, 8, BLK], BF16)
    t_mix = [T("mix%d" % i) for i in range(8)]
    s_ = [W("s%d" % i) for i in range(8)]
    sgw, t_sgw = s_[0]
    asg, t_asg = s_[1]
    kkk, t_kkk = s_[2]
    rn, t_rn = s_[3]
    kk, t_kk = s_[4]
    t1, t_t1 = s_[3]
    kmod, t_kmod = s_[2]
    bvec, t_bvec = s_[5]
    Lc, t_Lc = s_[3]
    Lr, t_Lr = s_[1]
    Lp, t_Lp = s_[7]
    E2, t_E2 = s_[3]
    E3, t_E3 = s_[0]
    hcv = [s_[i][0] for i in range(4)]
    t_hcv = [s_[i][1] for i in range(4)]
    cmu, t_cmu = s_[4]
    cvar, t_cvar = s_[5]
    ctmp, t_ctmp = s_[6]
    sgate, t_sgate = s_[7]
    sqb, t_sqb = W("sqb", dt=BF16)
    rkr, t_rkr = W("rkr", dt=BF16)
    ybf, t_ybf = W("ybf", dt=BF16)
    ysq, t_ysq = W("ysq", dt=BF16)
    IFT = [{"din": W("din%d" % p_, (128, 4)), "AR": W("AR%d" % p_, (128, 4, 2, 128), BF16), "BT": W("BT%d" % p_, dt=BF16),
            "KT": W("KT%d" % p_, dt=BF16), "E1": W("E1_%d" % p_), "bonus": W("bonus%d" % p_)} for p_ in range(2)]
    VT, t_VT = W("VT", dt=BF16)
    TOKM, t_TOKM = W("TOKM", (128, 4, 3, 128), BF16)
    G, t_G = W("G", (128, 4, 2, 2, 64), BF16)
    yT, t_yT = W("yT")
    Y0T, t_Y0T = W("Y0T")
    RhT, t_RhT = W("RhT", dt=BF16)
    MTs, t_MTs = W("MTs", (128, 4, 64), BF16)
    Npp, t_Npp = W("Npp", (128, 4, 64))
    DTH = [{nm: W(nm + str(h), (128, 4, 128), BF16) for nm in ("XTa", "XTb", "Xa", "Xb", "Pa", "Pb", "AakT")} for h in range(2)]
    HT = [{"ArbT": W("ArbT%d" % h, (128, 4, 128), BF16), "ArkT": W("ArkT%d" % h, (128, 4, 128), BF16),
           "F": W("F%d" % h, (128, 4, 2, 64), BF16)} for h in range(2)]
    _ar0 = IFT[0]["AR"][0].rearrange("p c a t -> p (c a t)")
    hcb = [_ar0[:, 0:512], _ar0[:, 512:1024], IFT[0]["BT"][0], IFT[0]["KT"][0]]
    t_hcb = [IFT[0]["AR"][1], IFT[0]["AR"][1], IFT[0]["BT"][1], IFT[0]["KT"][1]]
    _e10 = IFT[0]["E1"][0].bitcast(BF16)
    _bo0 = IFT[0]["bonus"][0].bitcast(BF16)
    hsq = [_e10[:, 0:512], _e10[:, 512:1024], _bo0[:, 0:512], _bo0[:, 512:1024]]
    t_hsq = [IFT[0]["E1"][1], IFT[0]["E1"][1], IFT[0]["bonus"][1], IFT[0]["bonus"][1]]
    Sm = sb("Sm", [128, NSEQ * 4, 64])
    t_Sm = [T("Sm%d" % i) for i in range(NSEQ * 4)]
    Zb = sb("Zb", [128, NSEQ * 4, 64], BF16)
    t_Zb = [T("Zb%d" % i) for i in range(NSEQ * 4)]
    S.op("dve", lambda e: e.memset(Sm, 0.0), [], t_Sm)
    hglu = sb("hglu", [128, 4, 30 + BLK], BF16)
    t_hglu = [T("hglu%d" % i) for i in range(4)]
    hhist = sb("hhist", [128, NSEQ * 4, 30], BF16)
    t_hhist = [T("hhist%d" % i) for i in range(NSEQ * 4)]
    S.op("dve", lambda e: e.memset(hhist, 0.0), [], t_hhist)
    x1t, t_x1t = W("x1t", (128, D))
    gmu, t_gmu = x1t[:, 0:512], t_x1t
    gvar, t_gvar = x1t[:, 512:1024], t_x1t
    gtmp, t_gtmp = Y0T, t_Y0T
    p12, t_p12 = Y0T, t_Y0T
    print("phase A SBUF top:", top[0])

    def col(base, i):
        return pcol[:, base + i:base + i + 1]

    def c4v(ap):
        return ap.rearrange("p (c t) -> p c t", c=4)

    def inproj_chunk(cc):
        b = big_bank()
        for k in range(8):
            S.op("pe", lambda e, k=k, b=b: e.matmul(PB[b][:, :], w_in_sb[:, k, cc * 128:(cc + 1) * 128], hT[:, k, :],
                                                     start=(k == 0), stop=(k == 7)),
                 [w_in_T(cc)] + t_hT, [tPB[b]])
        return b

    def shift_evac(b, cc, s, dst, t_dst):
        ci = s * 14 + cc
        S.op("act", lambda e: e.activation(tmpb[:, 1:BLK + 1], PB[b][:, :], AF.Copy, scale=pcol[:, PC_MU + cc:PC_MU + cc + 1]),
             [tPB[b], t_par], [t_tmpb])
        S.op("act", lambda e: e.copy(tmpb[:, 0:1], carry[:, ci:ci + 1]), [t_carry[ci], t_tmpb], [t_tmpb])
        S.op("act", lambda e: e.copy(carry[:, ci:ci + 1], tmpb[:, BLK:BLK + 1]), [t_tmpb], [t_carry[ci]])
        S.op("dve", lambda e: e.scalar_tensor_tensor(dst, PB[b][:, :], omm[:, cc:cc + 1], tmpb[:, 0:BLK], ALU.mult, ALU.add),
             [tPB[b], t_tmpb, t_cst], [t_dst])

    PS = 3

    def make_pair(par):
        AR, t_AR = IFT[par]['AR']
        BT, t_BT = IFT[par]['BT']
        KT, t_KT = IFT[par]['KT']
        E1, t_E1 = IFT[par]['E1']
        din, t_din = IFT[par]['din']
        bonus, t_bonus = IFT[par]['bonus']

        def rwkv_pair_prep(s, hp):
            for cc, dst, tdst in ((hp, pr, t_pr), (4 + hp, pk, t_pk), (8 + hp, pv, t_pv)):
                b = inproj_chunk(cc)
                shift_evac(b, cc, s, dst, tdst)
                yield
            S.op("pe", lambda e: e.matmul(PB[PS][:, :], lora_sb[0:64, hp * 128:(hp + 1) * 128], lw[0:64, :], start=True, stop=True),
                 [t_parA, t_lw], [tPB[PS]])
            S.op("act", lambda e: e.activation(sgw, PB[PS][:, :], AF.Sigmoid, bias=col(PC_W0, hp)), [tPB[PS], t_par], [t_sgw])
            yield
            S.op("pe", lambda e: e.matmul(PB[PS][:, :], lora_sb[64:128, hp * 128:(hp + 1) * 128], lw[64:128, :], start=True, stop=True),
                 [t_parA, t_lw], [tPB[PS]])
            S.op("act", lambda e: e.activation(asg, PB[PS][:, :], AF.Sigmoid, bias=col(PC_A0, hp)), [tPB[PS], t_par], [t_asg])
            yield
            S.op("dve", lambda e: e.tensor_scalar(kkk, pk, col(PC_KK, hp), None, ALU.mult), [t_pk, t_par], [t_kkk])
            yield
            S.op("act", lambda e: e.activation(sqb, kkk, AF.Square), [t_kkk], [t_sqb])
            yield
            S.op("pe", lambda e: e.matmul(PB[PS][:, :], bones_b, sqb, start=True, stop=True), [t_par, t_sqb], [tPB[PS]])
            S.op("act", lambda e: e.activation(rn, PB[PS][:, :], AF.Ln), [tPB[PS]], [t_rn])
            yield
            S.op("act", lambda e: e.activation(rn, rn, AF.Exp, scale=-0.5), [t_rn], [t_rn])
            yield
            S.op("dve", lambda e: e.tensor_tensor(kk, kkk, rn, ALU.mult), [t_kkk, t_rn], [t_kk])
            yield
            S.op("dve", lambda e: e.tensor_scalar(t1, asg, -1.0, col(PC_KA, hp), ALU.add, ALU.mult), [t_asg, t_par], [t_t1])
            yield
            S.op("dve", lambda e: e.scalar_tensor_tensor(kmod, t1, 1.0, pk, ALU.add, ALU.mult), [t_t1, t_pk], [t_kmod])
            yield
            S.op("dve", lambda e: e.tensor_tensor(bvec, kk, asg, ALU.mult), [t_kk, t_asg], [t_bvec])
            yield
            S.op("dve", lambda e: e.scalar_tensor_tensor(rkr, pr, col(PC_RK, hp), kmod, ALU.mult, ALU.mult),
                 [t_pr, t_kmod, t_par], [t_rkr])
            yield
            S.op("pe", lambda e: e.matmul(PB[PS][:, :], bones_b, rkr, start=True, stop=True), [t_par, t_rkr], [tPB[PS]])
            S.op("dve", lambda e: e.tensor_tensor(bonus, PB[PS][:, :], pv, ALU.mult), [tPB[PS], t_pv], [t_bonus])
            yield
            S.op("dve", lambda e: e.tensor_tensor_scan(Lc, ones_b, sgw, 0.0, ALU.mult, ALU.add), [t_sgw, t_par], [t_Lc])
            yield
            for c4 in range(4):
                cs = slice(c4 * 128, (c4 + 1) * 128)
                m = c4 * 128 + 63
                S.op("dve", lambda e, cs=cs, m=m: e.tensor_scalar(Lr[:, cs], Lc[:, cs], Lc[:, m:m + 1], None, ALU.subtract),
                     [t_Lc], [t_Lr])
                yield
            S.op("dve", lambda e: e.tensor_tensor(Lp, Lr, sgw, ALU.subtract), [t_Lr, t_sgw], [t_Lp])
            yield
            S.op("act", lambda e: e.activation(E1, Lr, AF.Exp, scale=-C0), [t_Lr], [t_E1])
            yield
            S.op("act", lambda e: e.activation(E2, Lr, AF.Exp, scale=C0), [t_Lr], [t_E2])
            yield
            S.op("act", lambda e: e.activation(E3, Lp, AF.Exp, scale=-C0), [t_Lp], [t_E3])
            yield
            S.op("act", lambda e: e.activation(din, c4v(Lp)[:, :, 0], AF.Exp, scale=C0), [t_Lp], [t_din])
            yield
            S.op("dve", lambda e: e.scalar_tensor_tensor(AR[:, :, 0, :], c4v(kk), -1.0, c4v(E3), ALU.mult, ALU.mult),
                 [t_kk, t_E3], [t_AR])
            yield
            S.op("dve", lambda e: e.tensor_tensor(AR[:, :, 1, :], c4v(pr), c4v(E1), ALU.mult), [t_pr, t_E1], [t_AR])
            yield
            S.op("dve", lambda e: e.tensor_tensor(BT, bvec, E2, ALU.mult), [t_bvec, t_E2], [t_BT])
            yield
            S.op("dve", lambda e: e.tensor_tensor(KT, kmod, E2, ALU.mult), [t_kmod, t_E2], [t_KT])
            yield
            S.op("act", lambda e: e.copy(VT, pv), [t_pv], [t_VT])
            yield

        def prep_tail():
            for c4 in range(4):
                cs = slice(c4 * 128, (c4 + 1) * 128)
                srcs = [(AR[:, c4, 0, :], t_AR), (BT[:, cs], t_BT), (KT[:, cs], t_KT), (VT[:, cs], t_VT)]
                for i, (src, tsrc) in enumerate(srcs):
                    S.op("pe", lambda e, src=src, i=i: e.transpose(PTB[:, i * 128:(i + 1) * 128], src, ident_b),
                         [tsrc, t_par], [tPTB])
                S.op("act", lambda e, c4=c4: e.copy(G[:, c4, :, 0, :], PTB[:, 0:128].rearrange("p (h k) -> p h k", h=2)),
                     [tPTB], [t_G])
                S.op("act", lambda e, c4=c4: e.copy(TOKM[:, c4, :, :], PTB[:, 128:512].rearrange("p (q k) -> p q k", q=3)),
                     [tPTB], [t_TOKM])

        def f2(ap):
            return ap.rearrange("p c t -> p (c t)")

        def rec_head(s, hp, h):
            hs = slice(h * 64, (h + 1) * 64)
            DT = DTH[h]
            XT0, tXT0 = DT["XTb"]
            X0, tX0 = DT["Xb"]
            AakT, tAakT = DT["AakT"]
            ArbT, tArbT = HT[h]["ArbT"]
            ArkT, tArkT = HT[h]["ArkT"]
            Fh, tF = HT[h]["F"]
            for (lt, tlt, o0, to0, o1, to1) in ((BT, t_BT, XT0, tXT0, ArbT, tArbT), (KT, t_KT, AakT, tAakT, ArkT, tArkT)):
                for half in range(2):
                    b = rec_bank()
                    for cq in range(2):
                        c4 = half * 2 + cq
                        S.op("pe", lambda e, b=b, c4=c4, cq=cq, lt=lt: e.matmul(
                            PB[b][:, cq * 256:(cq + 1) * 256], lt[hs, c4 * 128:(c4 + 1) * 128],
                            AR[hs, c4, :, :].rearrange("p a t -> p (a t)"), start=True, stop=True),
                            [tlt, t_AR], [tPB[b]])
                    pv4 = PB[b][:, :].rearrange("p (c a t) -> p c a t", c=2, a=2)
                    mv4 = maskT.rearrange("p (c a t) -> p c a t", c=2, a=2)
                    S.op("dve", lambda e, pv4=pv4, mv4=mv4, half=half, o0=o0: e.tensor_tensor(
                        o0[:, half * 2:half * 2 + 2, :], pv4[:, :, 0, :], mv4[:, :, 0, :], ALU.mult), [tPB[b], t_par], [to0])
                    S.op("dve", lambda e, pv4=pv4, mv4=mv4, half=half, o1=o1: e.tensor_tensor(
                        o1[:, half * 2:half * 2 + 2, :], pv4[:, :, 1, :], mv4[:, :, 1, :], ALU.mult), [tPB[b], t_par], [to1])
                    yield
            b = rec_bank()
            for c4 in range(4):
                S.op("pe", lambda e, b=b, c4=c4: e.matmul(PB[b][:, c4 * 128:(c4 + 1) * 128], AR[hs, c4, 0, :],
                                                           BT[hs, c4 * 128:(c4 + 1) * 128], start=True, stop=True),
                     [t_AR, t_BT], [tPB[b]])
            S.op("dve", lambda e, b=b: e.tensor_tensor(f2(X0), PB[b][:, :], maskL, ALU.mult), [tPB[b], t_par], [tX0])
            yield
            Pc, tPc = DT["Pa"]
            Pn, tPn = DT["Pb"]
            S.op("dve", lambda e, Pc=Pc: e.tensor_tensor(f2(Pc), f2(XT0), ident4_b, ALU.add), [tXT0, t_par], [tPc])
            Xc, tXc, XTc, tXTc = X0, tX0, XT0, tXT0
            nxt = [(DT["Xb"], DT["XTb"]), (DT["Xa"], DT["XTa"])]
            for j in range(1, 7):
                (Xn, tXn), (XTn, tXTn) = nxt[j % 2]
                if j < 6:
                    b = rec_bank()
                    for c4 in range(4):
                        S.op("pe", lambda e, b=b, c4=c4, Xc=Xc, XTc=XTc: e.matmul(PB[b][:, c4 * 128:(c4 + 1) * 128], Xc[:, c4, :],
                                                                                 XTc[:, c4, :], start=True, stop=True),
                             [tXc, tXTc], [tPB[b]])
                b2 = rec_bank()
                for c4 in range(4):
                    S.op("pe", lambda e, b2=b2, c4=c4, Xc=Xc, XTc=XTc: e.matmul(PB[b2][:, c4 * 128:(c4 + 1) * 128], XTc[:, c4, :],
                                                                               Xc[:, c4, :], start=True, stop=True),
                         [tXc, tXTc], [tPB[b2]])
                if j < 6:
                    S.op("act", lambda e, b=b, XTn=XTn: e.copy(f2(XTn), PB[b][:, :]), [tPB[b]], [tXTn])
                S.op("dve", lambda e, b2=b2, Xn=Xn: e.tensor_copy(f2(Xn), PB[b2][:, :]), [tPB[b2]], [tXn])
                yield
                b3 = rec_bank()
                for c4 in range(4):
                    S.op("pe", lambda e, b3=b3, c4=c4, Xn=Xn, Pc=Pc: e.matmul(PB[b3][:, c4 * 128:(c4 + 1) * 128], Xn[:, c4, :],
                                                                             Pc[:, c4, :], start=True, stop=True),
                         [tXn, tPc], [tPB[b3]])
                S.op("dve", lambda e, b3=b3, Pn=Pn, Pc=Pc: e.tensor_tensor(f2(Pn), PB[b3][:, :], f2(Pc), ALU.add),
                     [tPB[b3], tPc], [tPn])
                Pc, tPc, Pn, tPn = Pn, tPn, Pc, tPc
                Xc, tXc, XTc, tXTc = Xn, tXn, XTn, tXTn
                yield
            b = rec_bank()
            for c4 in range(4):
                S.op("pe", lambda e, b=b, c4=c4: e.matmul(PB[b][:, c4 * 64:(c4 + 1) * 64], AakT[:, c4, :], TOKM[:, c4, 2, hs],
                                                           start=True, stop=True), [tAakT, t_TOKM], [tPB[b]])
            S.op("act", lambda e, b=b: e.copy(G[:, :, h, 1, :], PB[b][:, 0:256].rearrange("p (c v) -> p c v", c=4)),
                 [tPB[b]], [t_G])
            yield
            b = rec_bank()
            for c4 in range(4):
                S.op("pe", lambda e, b=b, c4=c4, Pc=Pc: e.matmul(PB[b][:, c4 * 128:(c4 + 1) * 128], Pc[:, c4, :],
                                                                 G[:, c4, h, :, :].rearrange("p a k -> p (a k)"),
                                                                 start=True, stop=True), [tPc, t_G], [tPB[b]])
            S.op("act", lambda e, b=b: e.copy(Fh.rearrange("p c a k -> p (c a k)"), PB[b][:, :]), [tPB[b]], [tF])

        def rec_pair_finish(s, hp):
            si = s * 4 + hp
            bM, bN, bR, bY = rec_bank(), rec_bank(), rec_bank(), rec_bank()
            yield
            for h in range(2):
                hs = slice(h * 64, (h + 1) * 64)
                Fh, tF = HT[h]["F"]
                ArbT, tArbT = HT[h]["ArbT"]
                ArkT, tArkT = HT[h]["ArkT"]
                for c4 in range(4):
                    S.op("pe", lambda e, c4=c4, hs=hs, Fh=Fh: e.matmul(PB[bM][hs, c4 * 64:(c4 + 1) * 64], Fh[:, c4, 0, :],
                                                                       TOKM[:, c4, 0, hs], start=True, stop=True),
                         [tF, t_TOKM], [tPB[bM]])
                for c4 in range(4):
                    S.op("pe", lambda e, c4=c4, hs=hs, Fh=Fh: e.matmul(PB[bN][hs, c4 * 64:(c4 + 1) * 64], TOKM[:, c4, 0, hs],
                                                                       Fh[:, c4, 1, :], start=True, stop=False),
                         [tF, t_TOKM], [tPB[bN]])
                    S.op("pe", lambda e, c4=c4, hs=hs: e.matmul(PB[bN][hs, c4 * 64:(c4 + 1) * 64], TOKM[:, c4, 1, hs],
                                                                TOKM[:, c4, 2, hs], start=False, stop=True),
                         [t_TOKM], [tPB[bN]])
                for c4 in range(4):
                    S.op("pe", lambda e, c4=c4, hs=hs, Fh=Fh, ArbT=ArbT: e.matmul(PB[bR][hs, c4 * 128:(c4 + 1) * 128], Fh[:, c4, 0, :],
                                                                                  ArbT[:, c4, :], start=True, stop=True),
                         [tF, tArbT], [tPB[bR]])
                for c4 in range(4):
                    S.op("pe", lambda e, c4=c4, hs=hs, Fh=Fh, ArbT=ArbT: e.matmul(PB[bY][hs, c4 * 128:(c4 + 1) * 128], Fh[:, c4, 1, :],
                                                                                  ArbT[:, c4, :], start=True, stop=False),
                         [tF, tArbT], [tPB[bY]])
                    S.op("pe", lambda e, c4=c4, hs=hs, ArkT=ArkT: e.matmul(PB[bY][hs, c4 * 128:(c4 + 1) * 128], TOKM[:, c4, 2, hs],
                                                                           ArkT[:, c4, :], start=False, stop=True),
                         [t_TOKM, tArkT], [tPB[bY]])
            S.op("dve", lambda e: e.tensor_tensor(MTs.rearrange("p c k -> p (c k)"), PB[bM][:, 0:256], istack, ALU.add),
                 [tPB[bM], t_par], [t_MTs])
            for c4 in range(4):
                S.op("act", lambda e, c4=c4: e.activation(Npp[:, c4, :], PB[bN][:, c4 * 64:(c4 + 1) * 64], AF.Copy,
                                                          scale=E1[:, c4 * 128 + 127:c4 * 128 + 128]),
                     [tPB[bN], t_E1], [t_Npp])
            S.op("dve", lambda e: e.tensor_tensor(c4v(RhT), c4v(PB[bR][:, :]), AR[:, :, 1, :], ALU.add), [tPB[bR], t_AR], [t_RhT])
            S.op("act", lambda e: e.copy(Y0T, PB[bY][:, :]), [tPB[bY]], [t_Y0T])
            yield
            for c4 in range(4):
                S.op("act", lambda e, c4=c4: e.activation(Zb[:, si, :], Sm[:, si, :], AF.Copy, scale=din[:, c4:c4 + 1]),
                     [t_Sm[si], t_din], [t_Zb[si]])
                bz, by = rec_bank(), rec_bank()
                for h in range(2):
                    hs = slice(h * 64, (h + 1) * 64)
                    S.op("pe", lambda e, c4=c4, hs=hs, bz=bz: e.matmul(PB[bz][hs, 0:64], MTs[hs, c4, :], Zb[hs, si, :],
                                                                       start=True, stop=True), [t_MTs, t_Zb[si]], [tPB[bz]])
                    S.op("pe", lambda e, c4=c4, hs=hs, by=by: e.matmul(PB[by][hs, 0:128], Zb[hs, si, :],
                                                                       RhT[hs, c4 * 128:(c4 + 1) * 128], start=True, stop=True),
                         [t_RhT, t_Zb[si]], [tPB[by]])
                S.op("dve", lambda e, c4=c4, bz=bz: e.scalar_tensor_tensor(Sm[:, si, :], PB[bz][:, 0:64],
                                                                             E1[:, c4 * 128 + 127:c4 * 128 + 128], Npp[:, c4, :],
                                                                             ALU.mult, ALU.add),
                     [tPB[bz], t_E1, t_Npp], [t_Sm[si]])
                S.op("dve", lambda e, c4=c4, by=by: e.tensor_tensor(yT[:, c4 * 128:(c4 + 1) * 128], PB[by][:, 0:128],
                                                                     Y0T[:, c4 * 128:(c4 + 1) * 128], ALU.add),
                     [tPB[by], t_Y0T], [t_yT])
                yield

        def rwkv_pair_post(s, hp):
            S.op("act", lambda e: e.copy(ybf, yT), [t_yT], [t_ybf])
            S.op("act", lambda e: e.activation(ysq, yT, AF.Square), [t_yT], [t_ysq])
            S.op("pe", lambda e: e.matmul(PB[PS][:, :], bones_b, ybf, start=True, stop=True), [t_par, t_ybf], [tPB[PS]])
            S.op("dve", lambda e: e.tensor_scalar(gmu, PB[PS][:, :], 1.0 / 64, None, ALU.mult), [tPB[PS]], [t_gmu])
            S.op("pe", lambda e: e.matmul(PB[PS][:, :], bones_b, ysq, start=True, stop=True), [t_par, t_ysq], [tPB[PS]])
            S.op("dve", lambda e: e.tensor_tensor(gtmp, gmu, gmu, ALU.mult), [t_gmu], [t_gtmp])
            S.op("dve", lambda e: e.scalar_tensor_tensor(gvar, PB[PS][:, :], 1.0 / 64, gtmp, ALU.mult, ALU.subtract),
                 [tPB[PS], t_gtmp], [t_gvar])
            yield
            S.op("dve", lambda e: e.tensor_scalar(gvar, gvar, GN_EPS, None, ALU.add), [t_gvar], [t_gvar])
            S.op("act", lambda e: e.activation(gvar, gvar, AF.Ln), [t_gvar], [t_gvar])
            S.op("act", lambda e: e.activation(gvar, gvar, AF.Exp, scale=-0.5), [t_gvar], [t_gvar])
            S.op("dve", lambda e: e.tensor_tensor(gtmp, yT, gmu, ALU.subtract), [t_yT, t_gmu], [t_gtmp])
            S.op("dve", lambda e: e.tensor_tensor(gtmp, gtmp, gvar, ALU.mult), [t_gtmp, t_gvar], [t_gtmp])
            S.op("act", lambda e: e.activation(gtmp, gtmp, AF.Identity, scale=col(PC_LXG, hp), bias=col(PC_LXB, hp)),
                 [t_gtmp, t_par], [t_gtmp])
            S.op("dve", lambda e: e.tensor_tensor(gtmp, gtmp, bonus, ALU.add), [t_gtmp, t_bonus], [t_gtmp])
            S.op("pe", lambda e: e.matmul(PB[PS][:, :], gate_sb[:, hp * 128:(hp + 1) * 128], sg, start=True, stop=True),
                 [t_parA, t_sg], [tPB[PS]])
            S.op("dve", lambda e: e.tensor_tensor(mixT[:, hp, :], PB[PS][:, :], gtmp, ALU.mult), [t_gtmp, tPB[PS]], [t_mix[hp]])
            yield

        return dict(prep=rwkv_pair_prep, tail=prep_tail, head=rec_head, finish=rec_pair_finish, post=rwkv_pair_post)

    PAIR = [make_pair(0), make_pair(1)]

    conv_flag = [False]

    def conv_branch(s):
        for q in range(4):
            hi = s * 4 + q
            ba = inproj_chunk(14 + q)
            yield
            bg = inproj_chunk(18 + q)
            yield
            S.op("act", lambda e, bg=bg: e.activation(sgate, PB[bg][:, :], AF.Sigmoid), [tPB[bg]], [t_sgate])
            yield
            S.op("act", lambda e, hi=hi, q=q: e.copy(hglu[:, q, 0:30], hhist[:, hi, :]), [t_hhist[hi]], [t_hglu[q]])
            S.op("dve", lambda e, q=q, ba=ba: e.tensor_tensor(hglu[:, q, 30:30 + BLK], PB[ba][:, :], sgate, ALU.mult),
                 [tPB[ba], t_sgate], [t_hglu[q]])
            S.op("act", lambda e, hi=hi, q=q: e.copy(hhist[:, hi, :], hglu[:, q, BLK:BLK + 30]), [t_hglu[q]], [t_hhist[hi]])
            yield
        conv_flag[0] = True
        for q in range(4):
            b = big_bank()
            for j in range(CW):
                for g in range(2):
                    gs = slice(g * 64, (g + 1) * 64)
                    S.op("pe", lambda e, b=b, j=j, q=q, gs=gs: e.matmul(PB[b][gs, :], diagw[gs, q * CW + j, :],
                                                                       hglu[gs, q, j:j + BLK],
                                                                       start=(j == 0), stop=(j == CW - 1)),
                         [t_diag, t_hglu[q]], [tPB[b]])
                if j % 8 == 7:
                    yield
            S.op("act", lambda e, b=b, q=q: e.activation(hcv[q], PB[b][:, :], AF.Identity, bias=col(PC_CB, q)),
                 [tPB[b], t_par], [t_hcv[q]])
            yield
            S.op("act", lambda e, b=b, q=q: e.activation(hsq[q], PB[b][:, :], AF.Square, bias=col(PC_CB, q)),
                 [tPB[b], t_par], [t_hsq[q]])
            yield
            S.op("dve", lambda e, q=q: e.tensor_copy(hcb[q], hcv[q]), [t_hcv[q]], [t_hcb[q]])
            yield
        for q in range(4):
            S.op("pe", lambda e, q=q: e.matmul(PB[PS][:, :], aones_b, hcb[q], start=(q == 0), stop=(q == 3)),
                 [t_par, t_hcb[q]], [tPB[PS]])
        S.op("dve", lambda e: e.tensor_scalar(cmu, PB[PS][:, :], 1.0 / 512, None, ALU.mult), [tPB[PS]], [t_cmu])
        for q in range(4):
            S.op("pe", lambda e, q=q: e.matmul(PB[PS][:, :], aones_b, hsq[q], start=(q == 0), stop=(q == 3)),
                 [t_par, t_hsq[q]], [tPB[PS]])
        S.op("dve", lambda e: e.tensor_tensor(ctmp, cmu, cmu, ALU.mult), [t_cmu], [t_ctmp])
        S.op("dve", lambda e: e.scalar_tensor_tensor(cvar, PB[PS][:, :], 1.0 / 512, ctmp, ALU.mult, ALU.subtract),
             [tPB[PS], t_ctmp], [t_cvar])
        yield
        S.op("dve", lambda e: e.tensor_scalar(cvar, cvar, LN_EPS, None, ALU.add), [t_cvar], [t_cvar])
        yield
        S.op("act", lambda e: e.activation(cvar, cvar, AF.Ln), [t_cvar], [t_cvar])
        yield
        S.op("act", lambda e: e.activation(cvar, cvar, AF.Exp, scale=-0.5), [t_cvar], [t_cvar])
        yield
        for q in range(4):
            S.op("dve", lambda e, q=q: e.tensor_tensor(ctmp, hcv[q], cmu, ALU.subtract), [t_hcv[q], t_cmu], [t_ctmp])
            yield
            S.op("dve", lambda e, q=q: e.tensor_tensor(ctmp, ctmp, cvar, ALU.mult), [t_ctmp, t_cvar], [t_ctmp])
            yield
            S.op("act", lambda e, q=q: e.activation(mixT[:, 4 + q, :], ctmp, AF.Silu, scale=col(PC_CLG, q), bias=col(PC_CLB, q)),
                 [t_ctmp, t_par], [t_mix[4 + q]])
            yield

    def out_proj(s, blk):
        tok0 = s * SEQ + blk * BLK
        for j in range(4):
            sl = xs_n[0] % 2
            xs_n[0] += 1
            r0 = tok0 + j * 128
            dma("pool", XS[sl], x_d[r0:r0 + 128, :], [], [tXS[sl]], "xsp%d" % sl)
            yield
            for half in range(2):
                b = big_bank()
                for k in range(8):
                    S.op("pe", lambda e, b=b, k=k, j=j, half=half: e.matmul(PB[b][:, :], mixT[:, k, j * 128:(j + 1) * 128],
                                                                           w_out_sb[:, k, half * 512:(half + 1) * 512],
                                                                           start=(k == 0), stop=(k == 7)),
                         [t_w_out] + t_mix, [tPB[b]])
                S.op("dve", lambda e, b=b, sl=sl, half=half: e.tensor_tensor(x1t[:, half * 512:(half + 1) * 512], PB[b][:, :],
                                                                              XS[sl][:, half * 512:(half + 1) * 512], ALU.add),
                     [tPB[b], tXS[sl]], [t_x1t])
                yield
            dma("pool", x1_d[r0:r0 + 128, :], x1t, [t_x1t], [], "x1t")
            yield

    def drive(gens):
        gens = list(gens)
        while gens:
            for g in list(gens):
                try:
                    next(g)
                except StopIteration:
                    gens.remove(g)

    ONE = bool(os.environ.get("DEV_ONE"))

    def run_all(g):
        if g is None:
            return
        for _ in g:
            pass

    def R_gen(s, hp):
        fn = PAIR[hp % 2]
        gens = [fn["head"](s, hp, 0), fn["head"](s, hp, 1)]
        while gens:
            for g in list(gens):
                try:
                    next(g)
                except StopIteration:
                    gens.remove(g)
            yield
        yield from fn["finish"](s, hp)
        yield from fn["post"](s, hp)

    def lora_prep0_gen(s):
        b = inproj_chunk(12)
        shift_evac(b, 12, s, p12, t_p12)
        yield
        S.op("act", lambda e: e.activation(lw[0:64, :], p12[0:64, :], AF.Tanh), [t_p12], [t_lw])
        yield
        S.op("act", lambda e: e.copy(lw[64:128, :], p12[64:128, :]), [t_p12], [t_lw])
        yield
        b = inproj_chunk(13)
        shift_evac(b, 13, s, p12, t_p12)
        yield
        S.op("act", lambda e: e.activation(sg, p12, AF.Sigmoid), [t_p12], [t_sg])
        yield
        yield from PAIR[0]["prep"](s, 0)

    prefetched = [False]
    pre_done = [False]
    order = [(blk, s) for blk in range(NBLK) for s in range(NSEQ)]
    if ONE:
        order = [(0, 0)]
    if stage < 1:
        order = []
    for oi, (blk, s) in enumerate(order):
        tok0 = s * SEQ + blk * BLK
        if not prefetched[0]:
            run_all(front_gen(tok0, gA, t_parA, x_d))
        prefetched[0] = False
        if not pre_done[0]:
            run_all(lora_prep0_gen(s))
        pre_done[0] = False
        run_all(PAIR[0]["tail"]())
        nxt = order[oi + 1] if oi + 1 < len(order) else None
        for hp in range(4):
            gens = []
            if stage >= 2:
                gens.append(R_gen(s, hp))
            if hp < 3:
                gens.append(PAIR[(hp + 1) % 2]["prep"](s, hp + 1))
            elif stage >= 3:
                conv_flag[0] = False
                gens.append(conv_branch(s))
                if nxt is not None:
                    gens.append(front_gen(nxt[1] * SEQ + nxt[0] * BLK, gA, t_parA, x_d, wait_flag=conv_flag))
                    prefetched[0] = True
            drive(gens)
            if hp < 3:
                run_all(PAIR[(hp + 1) % 2]["tail"]())
        if stage >= 4:
            gens = [out_proj(s, blk)]
            if nxt is not None and prefetched[0]:
                gens.append(lora_prep0_gen(nxt[1]))
                pre_done[0] = True
            drive(gens)

    if debug_cols:
        DEBUG_HOOK(locals())
        build_program.dbg_names = dbg_names

    if stage >= 5:
        barrier()
        top[0] = persist_top
        BB = 256
        w1_sb = sb("w1_sb", [128, 8, DFF], BF16)
        w2_sb = sb("w2_sb", [128, 32, D], BF16)
        gB = sb("gB", [128, 2, D])
        f1T = sb("f1T", [128, 32, BB], BF16)
        t_f1 = [T("f1T%d" % j) for j in range(32)]
        rl, t_rl = W("rl", (128, BB))
        xk = [sb("xk%d" % j, [128, D]) for j in range(2)]
        t_xk = [T("xk%d" % j) for j in range(2)]
        ot = [sb("ot%d" % j, [128, D]) for j in range(2)]
        t_ot = [T("ot%d" % j) for j in range(2)]
        x2, t_x2 = W("x2", (128, D))
        print("phase B SBUF top:", top[0])
        t_w1 = [T("w1_%d" % k) for k in range(8)]
        t_w2 = [T("w2_%d" % k) for k in range(8)]
        t_parB = T("paramsB")
        dma("sp", gB[:, 0, :], gvec_d[1:2, :].broadcast_to([128, D]), [], [t_parB], "initB")
        dma("sp", gB[:, 1, :], gvec_d[2:3, :].broadcast_to([128, D]), [], [t_parB], "initB")
        w1_v = w_ff1_d.rearrange("(k p) n -> p k n", p=128)
        t_w1g = [T("w1g%d" % g) for g in range(8)]
        for g in range(8):
            dma("pool", w1_sb[:, :, g * 512:(g + 1) * 512], w1_v[:, :, g * 512:(g + 1) * 512], [], [t_w1g[g]], "w1g%d" % g,
                max_dma_last_dim=4096)
        w2_v = w_ff2_d.rearrange("(j p) n -> p j n", p=128)
        for k in range(8):
            dma("pool", w2_sb[:, k * 4:(k + 1) * 4, :], w2_v[:, k * 4:(k + 1) * 4, :], [], [t_w2[k]], "all:w2",
                max_dma_last_dim=4096)
        on = [0]
        rl2, t_rl2 = W("rl2", (128, BB))
        RL = [(rl, t_rl), (rl2, t_rl2)]
        F1RING = (0, 1, 3, 4)
        F2RING = (5, 6, 7)
        f1n = [0]
        f2n = [0]
        NB = TOK // BB

        def front(tb):
            par = tb % 2
            load_norm_transpose(tb * BB, gB[:, 0, :], t_parB, x1_d, 2, hoff=par * BB, tbase=par * 2)

        front(0)
        for tb in range(NB):
            tok0 = tb * BB
            par = tb % 2
            for j in range(32):
                b = F1RING[f1n[0] % 4]
                f1n[0] += 1
                for k in range(8):
                    S.op("pe", lambda e, b=b, k=k, j=j, par=par: e.matmul(PB[b][:, 0:BB], w1_sb[:, k, j * 128:(j + 1) * 128],
                                                                           hT[:, k, par * BB:(par + 1) * BB],
                                                                           start=(k == 0), stop=(k == 7)),
                         [t_w1g[j // 4], t_hT[par * 2], t_hT[par * 2 + 1]], [tPB[b]])
                rlj, t_rlj = RL[j % 2]
                S.op("act", lambda e, b=b, rlj=rlj: e.activation(rlj, PB[b][:, 0:BB], AF.Relu), [tPB[b]], [t_rlj])
                S.op("dve", lambda e, j=j, rlj=rlj: e.tensor_tensor(f1T[:, j, :], rlj, rlj, ALU.mult), [t_rlj], [t_f1[j]])
            if tb + 1 < NB:
                front(tb + 1)
            for jt in range(2):
                r0 = tok0 + jt * 128
                dma("sp", xk[jt], x1_d[r0:r0 + 128, :], [], [t_xk[jt]], "xk%d" % jt)
                for half in range(2):
                    b = F2RING[f2n[0] % 3]
                    f2n[0] += 1
                    for j in range(32):
                        S.op("pe", lambda e, b=b, j=j, jt=jt, half=half: e.matmul(PB[b][:, :], f1T[:, j, jt * 128:(jt + 1) * 128],
                                                                                 w2_sb[:, j, half * 512:(half + 1) * 512],
                                                                                 start=(j == 0), stop=(j == 31)),
                             [t_w2[j // 4], t_f1[j]], [tPB[b]])
                    S.op("dve", lambda e, b=b, jt=jt, half=half: e.tensor_tensor(x2[:, half * 512:(half + 1) * 512], PB[b][:, :],
                                                                                  xk[jt][:, half * 512:(half + 1) * 512], ALU.add),
                         [tPB[b], t_xk[jt]], [t_x2])
                osl = on[0] % 2
                on[0] += 1
                cj = ss4[:, 4 + jt:5 + jt]
                S.op("act", lambda e, osl=osl, cj=cj: e.activation(ot[osl], x2, AF.Square, accum_out=cj),
                     [t_x2], [t_ot[osl], t_ss[4 + jt]])
                rms_rstd(cj, t_ss[4 + jt], D, RMS_EPS)
                S.op("dve", lambda e, osl=osl, cj=cj: e.scalar_tensor_tensor(ot[osl], x2, cj, gB[:, 1, :], ALU.mult, ALU.mult),
                     [t_x2, t_ss[4 + jt], t_parB], [t_ot[osl]])
                dma("sp", out_d[r0:r0 + 128, :], ot[osl], [t_ot[osl]], [], "ot%d" % osl)

    S.finalize()

    sems = {}
    keys = set()
    for o in S.ops:
        if o.dkey is not None:
            keys.add(("dma", o.dkey))
    for e_ in Sched.ENGS:
        keys.add(("eng", e_))
    for k in sorted(keys):
        sems[k] = es.enter_context(nc.semaphore("s_%s_%s" % (k[0], k[1].replace(":", "_"))))
    dfinal = [(("dma", k), 16 * c) for k, c in S.dcount.items()]
    print("ops:", {e_: len(S.per[e_]) for e_ in Sched.ENGS}, "sems:", len(sems))
    block = es.enter_context(nc.Block())

    @block.sync
    def _(e):
        S.emit("sp", e, sems)
        for k, v in dfinal:
            e.wait_ge(sems[k], v)

    @block.scalar
    def _(e):
        S.emit("act", e, sems)

    @block.vector
    def _(e):
        S.emit("dve", e, sems)

    @block.gpsimd
    def _(e):
        S.emit("pool", e, sems)

    @block.tensor
    def _(e):
        S.emit("pe", e, sems)

    es.close()
    return nc


def DEBUG_HOOK(L):
    pass


def host_constants():
    eye = np.eye(128, dtype=np.float32)
    bo = np.zeros((128, 128), np.float32)
    bo[:64, :64] = 1.0
    bo[64:, 64:] = 1.0
    i = np.arange(128)[:, None]
    t = np.arange(128)[None, :]
    strictT = (t > i).astype(np.float32)
    inclT = (t >= i).astype(np.float32)
    strictL = (t < i).astype(np.float32)
    ist = (np.arange(128)[:, None] % 64 == np.arange(64)[None, :]).astype(np.float32)
    parts = [eye, bo, np.ones((128, 128), np.float32), eye, eye, eye, eye,
             strictT, inclT, strictT, inclT, strictL, strictL, strictL, strictL,
             ist, ist, ist, ist, np.ones((128, 512), np.float32)]
    cst = np.concatenate(parts, axis=1)
    assert cst.shape == (128, NCSTB), cst.shape
    return np.ascontiguousarray(cst)


def host_layout(inp):
    f = lambda a: np.ascontiguousarray(np.asarray(a, dtype=np.float32))
    colv = lambda v: f(v).reshape(-1, 128).T
    pcol = np.concatenate([
        colv(inp["shift_mu"][0]), colv(inp["w0"][0]), colv(inp["a0"][0]), colv(inp["k_k"][0]), colv(inp["k_a"][0]),
        colv(inp["r_k"][0]), colv(inp["ln_x_g"][0]), colv(inp["ln_x_b"][0]), colv(inp["conv_b"][0]),
        colv(inp["conv_ln_g"][0]), colv(inp["conv_ln_b"][0])], axis=1)
    assert pcol.shape == (128, NPCOL)
    cwt = f(inp["conv_w"][0]).T.reshape(4, 128, CW).transpose(1, 0, 2).reshape(128, 4 * CW)
    shared = {
        "w_in": f(inp["w_in"][0]), "w_out": f(inp["w_out"][0]), "w_ff1": f(inp["w_ff1"][0]), "w_ff2": f(inp["w_ff2"][0]),
        "lora_up": np.concatenate([f(inp["w_decay_up"][0]), f(inp["w_aaa_up"][0])], axis=0),
        "w_gate_up": f(inp["w_gate_up"][0]),
        "pcol": np.ascontiguousarray(pcol), "cw": np.ascontiguousarray(cwt),
        "gvec": np.stack([f(inp["norm_mix_g"][0]), f(inp["norm_mlp_g"][0]), f(inp["norm_final_g"])], axis=0),
        "cst": host_constants(),
    }
    x = f(inp["x"]).reshape(NCORES, TOK, D)
    return [dict(shared, x=x[c]) for c in range(NCORES)]


def kernel(**inputs):
    nc = build_program()
    in_maps = host_layout(inputs)
    res = run_bass_kernel_spmd(nc, in_maps, core_ids=list(range(NCORES)))
    out = np.stack([np.asarray(r["out"], dtype=np.float32) for r in res.results], axis=0)
    return out.reshape(16, SEQ, D)
```
